# Optimizing a Trainium2 kernel written in Bass

```python
import math
import jax, jax.numpy as jnp
from jax import lax
import numpy as np

D_MODEL = 1024
BATCH = 8
SEQ = 4096
DEPTH = 1

N_MEM = 256
MEM_HEADS = 4
MEM_HEAD_DIM = 128
MEM_WIDTH = MEM_HEADS * MEM_HEAD_DIM
D_RNN = 3 * D_MODEL // 4
LRU_BLOCK = 64
N_LRU_BLOCKS = D_RNN // LRU_BLOCK
CONV_WIDTH = 4
LRU_C = 8.0
DIL_GROUPS = ((128, 1), (512, 4), (2048, 16))
N_DIL_GROUPS = len(DIL_GROUPS)
HEADS_PER_GROUP = 4
DIL_HEAD_DIM = 64
N_DIL_HEADS = N_DIL_GROUPS * HEADS_PER_GROUP
DIL_QKV_WIDTH = 3 * N_DIL_HEADS * DIL_HEAD_DIM
DIL_OUT_WIDTH = HEADS_PER_GROUP * DIL_HEAD_DIM
NUM_BUCKETS = 32
MAX_DISTANCE = 2048
N_BRANCHES = 3
D_FF = 4 * D_MODEL
EPS = 1e-6
NEG = -1e30

SPLITS = (D_RNN, 2 * D_RNN, 2 * D_RNN + DIL_QKV_WIDTH, 2 * D_RNN + DIL_QKV_WIDTH + MEM_WIDTH)
D_IN = 2 * D_RNN + DIL_QKV_WIDTH + MEM_WIDTH + N_BRANCHES * D_MODEL

kernel_name = "hybrid_rglru_dilated_attn_memxattn_block"


def _rmsnorm(x, g):
    x32 = x.astype(jnp.float32)
    y = x32 * lax.rsqrt(jnp.mean(x32 * x32, axis=-1, keepdims=True) + EPS)
    return (y * g.astype(jnp.float32)).astype(x.dtype)


def _t5_bucket(dist):
    max_exact = NUM_BUCKETS // 2
    df = jnp.maximum(dist, 1).astype(jnp.float32)
    large = max_exact + (jnp.log(df / max_exact) / math.log(MAX_DISTANCE / max_exact)
                         * (NUM_BUCKETS - max_exact)).astype(jnp.int32)
    large = jnp.minimum(large, NUM_BUCKETS - 1)
    return jnp.where(dist < max_exact, dist, large)


def _rg_lru(xc, w_a, b_a, w_x, b_x, lam):
    B, S, C = xc.shape
    xb = xc.reshape(B, S, N_LRU_BLOCKS, LRU_BLOCK)
    r = jax.nn.sigmoid((jnp.einsum('bshi,hij->bshj', xb, w_a).reshape(B, S, C) + b_a).astype(jnp.float32))
    i = jax.nn.sigmoid((jnp.einsum('bshi,hij->bshj', xb, w_x).reshape(B, S, C) + b_x).astype(jnp.float32))
    log_a = -LRU_C * r * jax.nn.softplus(-lam.astype(jnp.float32))
    a = jnp.exp(log_a)
    mult = jnp.sqrt(-jnp.expm1(2.0 * log_a))
    b = mult * (i * xc.astype(jnp.float32))

    def combine(left, right):
        a1, b1 = left
        a2, b2 = right
        return a1 * a2, a2 * b1 + b2

    _, h = lax.associative_scan(combine, (a, b), axis=1)
    return h


def _dilated_group(q, k, v, table_g, dilation, span):
    B, S, H, Dh = q.shape
    L = S // dilation
    nb = -(-L // span)
    Lp = nb * span

    def to_blocks(t):
        t = t.reshape(B, L, dilation, H, Dh).transpose(0, 2, 3, 1, 4)
        t = jnp.pad(t, ((0, 0), (0, 0), (0, 0), (0, Lp - L), (0, 0)))
        return t.reshape(B, dilation, H, nb, span, Dh)

    def with_prev(t):
        prev = jnp.pad(t, ((0, 0), (0, 0), (0, 0), (1, 0), (0, 0), (0, 0)))[:, :, :, :-1]
        return jnp.concatenate([prev, t], axis=-2)

    qb = to_blocks(q)
    kk = with_prev(to_blocks(k))
    vv = with_prev(to_blocks(v))

    qi = jnp.arange(span)[:, None]
    kj = jnp.arange(2 * span)[None, :]
    off = qi + span - kj
    valid = (off >= 0) & (off <= span)
    mask = valid[None] & ((jnp.arange(nb)[:, None, None] > 0) | (kj >= span)[None])
    bucket = _t5_bucket(jnp.maximum(off, 0) * dilation)
    bias = jnp.transpose(table_g.astype(jnp.float32)[bucket], (2, 0, 1))

    s = jnp.einsum('brhnqc,brhnkc->brhnqk', qb, kk) * (DIL_HEAD_DIM ** -0.5) + bias[None, None, :, None]
    s = jnp.where(mask[None, None, None], s, NEG)
    m = jnp.max(s, axis=-1, keepdims=True)
    p = jnp.exp(s - m)
    den = jnp.sum(p, axis=-1)
    o = jnp.einsum('brhnqk,brhnkc->brhnqc', p, vv) / den[..., None]
    lse = m[..., 0] + jnp.log(den)

    o = o.reshape(B, dilation, H, Lp, Dh)[:, :, :, :L].transpose(0, 3, 1, 2, 4).reshape(B, S, H, Dh)
    lse = lse.reshape(B, dilation, H, Lp)[:, :, :, :L].transpose(0, 3, 1, 2).reshape(B, S, H)
    return o, lse


def setup_inputs(seed: int = 0) -> dict:
    key = jax.random.key(seed)
    ks = jax.random.split(key, 24)
    f32 = jnp.float32

    def nrm(k, shape, fan_in):
        return jax.random.normal(k, shape, f32) * (fan_in ** -0.5)

    def gain(k, n):
        return 1.0 + 0.05 * jax.random.normal(k, (n,), f32)

    u = jax.random.uniform(ks[11], (D_RNN,), f32, 0.9, 0.999)
    sig = u ** (1.0 / LRU_C)
    lam = jnp.log(sig) - jnp.log1p(-sig)
    return {
        "x": jax.random.normal(ks[0], (BATCH, SEQ, D_MODEL), f32),
        "mem": jax.random.normal(ks[1], (BATCH, N_MEM, D_MODEL), f32),
        "g_mix": gain(ks[2], D_MODEL),
        "w_in": nrm(ks[3], (D_MODEL, D_IN), D_MODEL),
        "b_gate": 0.01 * jax.random.normal(ks[4], (N_BRANCHES * D_MODEL,), f32),
        "conv_w": nrm(ks[5], (CONV_WIDTH, D_RNN), CONV_WIDTH),
        "conv_b": 0.01 * jax.random.normal(ks[6], (D_RNN,), f32),
        "w_rg_a": nrm(ks[7], (N_LRU_BLOCKS, LRU_BLOCK, LRU_BLOCK), LRU_BLOCK),
        "b_rg_a": 0.01 * jax.random.normal(ks[8], (D_RNN,), f32),
        "w_rg_x": nrm(ks[9], (N_LRU_BLOCKS, LRU_BLOCK, LRU_BLOCK), LRU_BLOCK),
        "b_rg_x": 0.01 * jax.random.normal(ks[10], (D_RNN,), f32),
        "lru_lambda": lam,
        "w_lru_out": nrm(ks[12], (D_RNN, D_MODEL), D_RNN),
        "rel_bias": 0.1 * jax.random.normal(ks[13], (NUM_BUCKETS, N_DIL_HEADS), f32),
        "w_dil_out": nrm(ks[14], (DIL_OUT_WIDTH, D_MODEL), DIL_OUT_WIDTH),
        "g_mem": gain(ks[15], D_MODEL),
        "w_mem_kv": nrm(ks[16], (D_MODEL, 2 * MEM_WIDTH), D_MODEL),
        "w_mem_out": nrm(ks[17], (MEM_WIDTH, D_MODEL), MEM_WIDTH),
        "w_out": nrm(ks[18], (D_MODEL, D_MODEL), D_MODEL),
        "g_mlp": gain(ks[19], D_MODEL),
        "w_mlp_in": nrm(ks[20], (D_MODEL, D_FF), D_MODEL),
        "w_mlp_out": nrm(ks[21], (D_FF, D_MODEL), D_FF),
        "g_final": gain(ks[22], D_MODEL),
    }


def reference(x, mem, g_mix, w_in, b_gate, conv_w, conv_b, w_rg_a, b_rg_a, w_rg_x, b_rg_x,
              lru_lambda, w_lru_out, rel_bias, w_dil_out, g_mem, w_mem_kv, w_mem_out, w_out,
              g_mlp, w_mlp_in, w_mlp_out, g_final):
    B, S, D = x.shape
    dt = x.dtype
    f32 = jnp.float32
    mem_n = _rmsnorm(mem, g_mem)

    for _ in range(DEPTH):
        h = _rmsnorm(x, g_mix)
        proj = h @ w_in
        x_lru, gate_lru, qkv, q_mem, gates = jnp.split(proj, SPLITS, axis=-1)

        xc = lax.conv_general_dilated(x_lru, conv_w[:, None, :].astype(x_lru.dtype), window_strides=(1,),
                                      padding=[(CONV_WIDTH - 1, 0)], dimension_numbers=('NWC', 'WIO', 'NWC'),
                                      feature_group_count=D_RNN) + conv_b
        hl = _rg_lru(xc, w_rg_a, b_rg_a, w_rg_x, b_rg_x, lru_lambda)
        y_lru = (jax.nn.gelu(gate_lru.astype(f32)) * hl).astype(dt) @ w_lru_out

        qkv = qkv.astype(f32).reshape(B, S, 3, N_DIL_GROUPS, HEADS_PER_GROUP, DIL_HEAD_DIM)
        outs, lses = [], []
        for g, (window, dil) in enumerate(DIL_GROUPS):
            o_g, lse_g = _dilated_group(qkv[:, :, 0, g], qkv[:, :, 1, g], qkv[:, :, 2, g],
                                        rel_bias[:, g * HEADS_PER_GROUP:(g + 1) * HEADS_PER_GROUP],
                                        dil, window // dil)
            outs.append(o_g)
            lses.append(lse_g)
        alpha = jax.nn.softmax(jnp.stack(lses, axis=0), axis=0)
        o_dil = jnp.sum(alpha[..., None] * jnp.stack(outs, axis=0), axis=0)
        y_dil = o_dil.reshape(B, S, DIL_OUT_WIDTH).astype(dt) @ w_dil_out

        kv = (mem_n @ w_mem_kv).astype(f32).reshape(B, N_MEM, 2, MEM_HEADS, MEM_HEAD_DIM)
        qm = q_mem.astype(f32).reshape(B, S, MEM_HEADS, MEM_HEAD_DIM)
        sm = jnp.einsum('bqhc,bkhc->bhqk', qm, kv[:, :, 0]) * (MEM_HEAD_DIM ** -0.5)
        pm = jax.nn.softmax(sm, axis=-1)
        om = jnp.einsum('bhqk,bkhc->bqhc', pm, kv[:, :, 1]).reshape(B, S, MEM_WIDTH)
        y_mem = om.astype(dt) @ w_mem_out

        gt = jax.nn.sigmoid((gates + b_gate).astype(f32)).reshape(B, S, N_BRANCHES, D)
        merged = (gt[:, :, 0] * y_lru.astype(f32) + gt[:, :, 1] * y_dil.astype(f32)
                  + gt[:, :, 2] * y_mem.astype(f32)).astype(dt)
        x = x + merged @ w_out

        hm = _rmsnorm(x, g_mlp)
        x = x + jnp.square(jax.nn.relu(hm @ w_mlp_in)) @ w_mlp_out

    return _rmsnorm(x, g_final)
```

```python
import math
import numpy as np
import concourse.bass as bass
import concourse.mybir as mybir
from concourse.bass_utils import run_bass_kernel_spmd

F32 = mybir.dt.float32
BF16 = mybir.dt.bfloat16
AF = mybir.ActivationFunctionType
ALU = mybir.AluOpType

D = 1024
SEQ = 4096
T = 512
NT = SEQ // T
NEG = -1e30
DILS = (1, 4, 16)
SBUF_BYTES = 229376
U_KB = 44
UOFF = SBUF_BYTES - (U_KB + 8) * 1024


class Res:
    __slots__ = ("name", "w", "r", "dsem", "dcnt", "dsem2", "dcnt2")

    def __init__(self, name):
        self.name = name
        self.w = None
        self.r = []
        self.dsem = None
        self.dcnt = 0
        self.dsem2 = None
        self.dcnt2 = 0


class Buf:
    def __init__(self, t, R):
        self.t = t
        self.R = R


def _flat(xs):
    out = []
    for x in xs:
        if isinstance(x, Res):
            out.append(x)
        elif isinstance(x, Buf):
            out.extend(x.R)
        else:
            out.extend(_flat(x))
    return out


class Sched:
    ENG = ("pe", "act", "dve", "pool", "sp")
    SAME_ENG_SYNC = ("act", "dve", "pool")

    def __init__(self, nc):
        self.nc = nc
        self.streams = {e: [] for e in self.ENG}
        self.count = {e: 0 for e in self.ENG}
        self.waited = {e: {} for e in self.ENG}
        self.sem = {}
        for e in self.ENG:
            self.sem["e_" + e] = nc.alloc_semaphore("sem_" + e)
        self.ndsem = 0
        self.phase = 'setup'
        self.tags = {e: [] for e in self.ENG}

    def _collect(self, eng, reads, writes):
        deps = {}

        def need(ev):
            if ev is None:
                return
            s, v = ev
            if deps.get(s, 0) < v:
                deps[s] = v

        for r in reads:
            need(r.w)
        for w in writes:
            need(w.w)
            for ev in w.r:
                need(ev)
        waits = []
        for s, v in deps.items():
            if s == "e_" + eng and eng not in self.SAME_ENG_SYNC:
                continue
            if self.waited[eng].get(s, 0) >= v:
                continue
            self.waited[eng][s] = v
            waits.append((s, v))
        return waits

    def _post(self, ev, reads, writes):
        for r in reads:
            r.r.append(ev)
            if len(r.r) > 12:
                best = {}
                for s, v in r.r:
                    if best.get(s, 0) < v:
                        best[s] = v
                r.r = list(best.items())
        for w in writes:
            w.w = ev
            w.r = []

    def op(self, eng, fn, reads=(), writes=()):
        reads = _flat(reads)
        writes = _flat(writes)
        waits = self._collect(eng, reads, writes)
        self.count[eng] += 1
        ev = ("e_" + eng, self.count[eng])
        self.streams[eng].append((waits, fn, ev[0], 1))
        self.tags[eng].append(self.phase)
        self._post(ev, reads, writes)

    def dma(self, q, fns, reads=(), writes=()):
        reads = _flat(reads)
        writes = _flat(writes)
        dst = writes[0]
        sw = (q == "pool")
        if (dst.dsem2 if sw else dst.dsem) is None:
            nm = "d_%d" % self.ndsem
            self.sem[nm] = self.nc.alloc_semaphore("dsem%d" % self.ndsem)
            self.ndsem += 1
            if sw:
                dst.dsem2 = nm
            else:
                dst.dsem = nm
        waits = self._collect(q, reads, writes)
        if sw:
            dst.dcnt2 += 16 * len(fns)
            ev = (dst.dsem2, dst.dcnt2)
        else:
            dst.dcnt += 16 * len(fns)
            ev = (dst.dsem, dst.dcnt)
        for i, fn in enumerate(fns):
            self.streams[q].append((waits if i == 0 else [], fn, ev[0], 16))
        self._post(ev, reads, writes)

    def wait_for(self, eng, ress):
        waits = self._collect(eng, _flat(ress), ())
        self.streams[eng].append((waits, None, None, 0))

    def emit(self):
        names = {"pe": "tensor", "act": "scalar", "dve": "vector", "pool": "gpsimd", "sp": "sync"}
        with self.nc.Block() as block:
            for eng in self.ENG:
                def f(e, eng=eng):
                    for waits, fn, semname, inc in self.streams[eng]:
                        for (s, v) in waits:
                            e.wait_ge(self.sem[s], v)
                        if fn is not None:
                            fn(e).then_inc(self.sem[semname], inc)
                getattr(block, names[eng])(f)


def _t5_bucket(dist):
    dist = np.asarray(dist, dtype=np.int64)
    df = np.maximum(dist, 1).astype(np.float32)
    val = np.log(df / np.float32(16.0)) / np.float32(math.log(2048 / 16)) * np.float32(16.0)
    large = 16 + val.astype(np.float32).astype(np.int32)
    large = np.minimum(large, 31)
    return np.where(dist < 16, dist, large)


def _host_consts():
    oh = np.zeros((6, 32, 256), np.float32)
    mask = np.zeros((2, 256), np.float32)
    for g, dil in enumerate(DILS):
        for part in range(2):
            for m in range(255):
                d = m - 127
                if part == 0:
                    valid, off = d <= 0, d + 128
                else:
                    valid, off = d >= 0, d
                if valid:
                    oh[g * 2 + part, int(_t5_bucket(off * dil)), m] = 1.0
                else:
                    mask[part, m] = NEG
    return oh, mask


def build_program(ntiles=NT, dbg=(), stop=None):
    nc = bass.Bass("TRN2", target_bir_lowering=False)
    S = Sched(nc)

    def din(name, shape):
        return nc.dram_tensor(name, list(shape), F32, kind="ExternalInput")

    x = din("x", [SEQ, D]); mem = din("mem", [256, D])
    g_mix = din("g_mix", [D]); w_in = din("w_in", [D, 7424]); b_gate = din("b_gate", [3072])
    conv_w = din("conv_w", [4, 768]); conv_b = din("conv_b", [768])
    w_rg_a = din("w_rg_a", [12, 64, 64]); b_rg_a = din("b_rg_a", [768])
    w_rg_x = din("w_rg_x", [12, 64, 64]); b_rg_x = din("b_rg_x", [768])
    lru_lambda = din("lru_lambda", [768]); w_lru_out = din("w_lru_out", [768, D])
    rel_bias = din("rel_bias", [32, 12]); w_dil_out = din("w_dil_out", [256, D])
    g_mem = din("g_mem", [D]); w_mem_kv = din("w_mem_kv", [D, D]); w_mem_out = din("w_mem_out", [512, D])
    w_out = din("w_out", [D, D]); g_mlp = din("g_mlp", [D]); w_mlp_in = din("w_mlp_in", [D, 4096])
    w_mlp_out = din("w_mlp_out", [4096, D]); g_final = din("g_final", [D])
    c_oh = din("c_oh", [6, 32, 256]); c_mask = din("c_mask", [2, 256])
    y = nc.dram_tensor("y", [SEQ, D], F32, kind="ExternalOutput")
    R_y = Res("y")
    dbg_out = {}

    def SB(name, shape, dt):
        return Buf(nc.alloc_sbuf_tensor(name, list(shape), dt), [Res(name)])

    x_tms = [SB("x_tm0", [128, 4, D], F32), SB("x_tm1", [128, 4, D], F32)]
    for _xb in x_tms:
        _xb.R = [Res(_xb.R[0].name + "_s%d" % i) for i in range(4)]
    x_tm = x_tms[0]
    xn = SB("xn", [128, 4, D], BF16)
    ss = SB("ss", [128, 4], F32)
    rstd = SB("rstd", [128, 4], F32)
    hT = SB("hT", [128, 8, T], BF16)
    ident = SB("ident", [128, 128], BF16)
    ones = SB("ones", [128, 128], BF16)
    gmix = SB("gmix", [128, 8], F32); gmem = SB("gmem", [128, 8], F32); gmlp = SB("gmlp", [128, 8], F32)
    bgate = SB("bgate", [128, 24], F32)
    cw = SB("cw", [128, 6, 4], F32); cb = SB("cb", [128, 6], F32)
    ba_c = SB("ba_c", [128, 6], F32); bx_c = SB("bx_c", [128, 6], F32)
    lam = SB("lam", [128, 6], F32); c8 = SB("c8", [128, 6], F32); c4 = SB("c4", [128, 6], F32)
    hba = SB("hba", [128, 6], F32); hbx = SB("hbx", [128, 6], F32)
    gfin = SB("gfin", [128, D], F32)
    bda = SB("bda", [128, 6, 128], BF16); bdx = SB("bdx", [128, 6, 128], BF16)
    carry = SB("carry", [128, 6, 3], F32); hstate = SB("hstate", [128, 6], F32)
    kmT = SB("kmT", [128, 4, 256], BF16); vm = SB("vm", [128, 2, 512], BF16)
    kT0 = SB("kT0", [128, 2, 1024], BF16); kT1 = SB("kT1", [128, 2, 1024], BF16); kT2 = SB("kT2", [128, 2, SEQ], BF16)
    V0 = SB("V0", [128, 8, 256], BF16); V1 = SB("V1", [128, 8, 256], BF16); V2 = SB("V2", [128, 32, 256], BF16)
    B_all = SB("B_all", [128, 24, 128], BF16)
    NSLOT = 5
    wslots = [SB("wslot%d" % i, [128, 4096], BF16) for i in range(NSLOT)]
    assert nc.sbuf_bytes_remaining >= (U_KB + 8) * 1024 + 64, nc.sbuf_bytes_remaining

    R_U = [Res("U%d" % i) for i in range(U_KB)]

    def UB(name, shape, dt, off_kb):
        nbytes = int(np.prod(shape[1:])) * (4 if dt == F32 else 2)
        t = nc.alloc_sbuf_tensor_at(name, list(shape), dt, offset=UOFF + off_kb * 1024)
        return Buf(t, R_U[off_kb: off_kb + (nbytes + 1023) // 1024])

    identf = UB("identf", [128, 128], F32, 0); bdf = UB("bdf", [128, 6, 128], F32, 1)
    ohsb = UB("ohsb", [32, 6, 256], F32, 4); maskb = UB("maskb", [4, 2, 256], F32, 10)
    fsb = UB("fsb", [4, 6, 256], F32, 12); tab = UB("tab", [32, 12], F32, 18)
    gated = UB("gated", [128, 6, T], BF16, 0)
    qT = UB("qT", [128, 6, T], BF16, 6)
    xl2 = [UB("xl0", [128, 515], F32, 12), UB("xl1", [128, 515], F32, 15)]
    xc2 = [UB("xc0", [128, T], F32, 18), UB("xc1", [128, T], F32, 20)]
    xcb2 = [UB("xcb0", [128, T], BF16, 22), UB("xcb1", [128, T], BF16, 23)]
    gl2 = [UB("gl0", [128, T], F32, 24), UB("gl1", [128, T], F32, 26)]
    thr2 = [UB("thr0", [128, T], F32, 28), UB("thr1", [128, T], F32, 30)]
    thi2 = [UB("thi0", [128, T], F32, 32), UB("thi1", [128, T], F32, 34)]
    m2 = [UB("m0", [128, T], F32, 36), UB("m1", [128, T], F32, 38)]
    hl = UB("hl", [128, T], F32, 40)
    V2tmp = [UB("V2tmp%d" % i, [128, 256], BF16, 42) for i in range(1)]
    _v2h = nc.alloc_sbuf_tensor_at("V2tmp_all", [128, 4, 256], BF16, offset=UOFF + 42 * 1024)
    V2tmp = [Buf(_v2h[:, i, :], [R_U[42 + i // 2]]) for i in range(4)]
    qmT = UB("qmT", [128, 4, T], BF16, 12)
    omT = UB("omT", [128, 4, T], BF16, 16)
    odT = UB("odT", [128, 2, T], BF16, 20)
    PTs = [[UB("PT0%d" % i, [128, T], BF16, 22 + i) for i in range(3)],
           [UB("PT1%d" % i, [128, T], BF16, 25 + i) for i in range(3)]]
    NUM = UB("NUM", [128, T], F32, 30); DEN = UB("DEN", [128, T], F32, 32); rden = UB("rden", [128, T], F32, 34)
    pTm = [UB("pTm0", [128, T], BF16, 36), UB("pTm1", [128, T], BF16, 37)]
    rscr = UB("rscr", [128, T], F32, 28)
    rdm = rden
    sgbufs = [UB("sg0", [128, T], F32, 6), UB("sg1", [128, T], F32, 12)]
    tmpbufs = [UB("tmpm0", [128, T], F32, 8), UB("tmpm1", [128, T], F32, 10)]
    acc = UB("acc", [128, 4, T], F32, 22); mergedbf = UB("mergedbf", [128, 8, T], BF16, 30)
    uT = [UB("uT%d" % i, [128, T], BF16, i) for i in range(32)]
    rl = [UB("rl0", [128, T], F32, 32), UB("rl1", [128, T], F32, 34)]

    banks = [nc.alloc_psum_tensor("ps%d" % i, [128, 512], F32) for i in range(8)]
    R_b = [Res("ps%d" % i) for i in range(8)]
    pstate = [0]

    def PN():
        i = pstate[0] % 8
        pstate[0] += 1
        return i

    sub = {"st": [0, (0, 1, 2, 3)], "nd": [0, (4, 5, 6, 7)]}

    def PNS(kind):
        st = sub[kind]
        i = st[1][st[0] % len(st[1])]
        st[0] += 1
        return i

    def ACT(out, in_, func, reads, writes, bias=None, scale=None, accum=None):
        kw = {}
        if bias is not None:
            kw["bias"] = bias
        if scale is not None:
            kw["scale"] = scale
        if accum is not None:
            kw["accum_out"] = accum
        S.op("act", lambda e: e.activation(out=out, in_=in_, func=func, **kw), reads, writes)

    def TS(eng, out, in0, s1, s2, op0, op1, reads, writes):
        S.op(eng, lambda e: e.tensor_scalar(out=out, in0=in0, scalar1=s1, scalar2=s2, op0=op0, op1=op1), reads, writes)

    def TT(eng, out, in0, in1, op, reads, writes):
        S.op(eng, lambda e: e.tensor_tensor(out=out, in0=in0, in1=in1, op=op), reads, writes)

    def STT(out, in0, scalar, in1, op0, op1, reads, writes):
        S.op("dve", lambda e: e.scalar_tensor_tensor(out=out, in0=in0, scalar=scalar, in1=in1, op0=op0, op1=op1), reads, writes)

    def CP(eng, out, in_, reads, writes):
        S.op(eng, lambda e: e.tensor_copy(out, in_), reads, writes)

    def MM(b, out, lhsT, rhs, start, stop, reads):
        S.op("pe", lambda e: e.matmul(out, lhsT=lhsT, rhs=rhs, start=start, stop=stop), reads, [R_b[b]])

    def DMA(q, pairs, reads, writes, slow=False):
        fns = []
        for (o, i) in pairs:
            if slow:
                fns.append(lambda e, o=o, i=i: e.dma_start(out=o, in_=i, allow_slow_non_contiguous=True))
            else:
                fns.append(lambda e, o=o, i=i: e.dma_start(out=o, in_=i))
        S.dma(q, fns, reads, writes)

    S.op("pool", lambda e: e.memset(identf.t[:], 1.0), [], [identf])
    S.op("pool", lambda e: e.affine_select(out=identf.t[:], in_=identf.t[:], pattern=[[-1, 128]], compare_op=ALU.is_equal,
                                           fill=0.0, base=0, channel_multiplier=1), [identf], [identf])
    CP("dve", ident.t[:], identf.t[:], [identf], [ident])
    S.op("dve", lambda e: e.memset(ones.t[:], 1.0), [], [ones])
    for buf in (kT2, V2, carry, hstate, bdf):
        S.op("pool", lambda e, buf=buf: e.memset(buf.t[:], 0.0), [], [buf])

    pieces = {}

    def kp(ap):
        return ap.rearrange("(k p) n -> p k n", p=128)

    def add_piece(name, K, cols, srcs):
        h = nc.dram_tensor("wsc_" + name, [128, K * cols], BF16)
        R = Res("wsc_" + name)
        pieces[name] = (h, K, cols, R, srcs)

    wi = w_in.ap()
    tile_order = []
    add_piece("mkv0", 8, 512, [(0, 512, w_mem_kv.ap()[:, 0:512])])
    add_piece("mkv1", 8, 512, [(0, 512, w_mem_kv.ap()[:, 512:1024])])
    for cp in range(3):
        add_piece("L%d" % cp, 8, 512, [(0, 256, wi[:, cp * 256:(cp + 1) * 256]), (256, 256, wi[:, 768 + cp * 256:768 + (cp + 1) * 256])])
    for pc in range(3):
        add_piece("QK%d" % pc, 8, 512, [(0, 512, wi[:, 1536 + pc * 512:1536 + (pc + 1) * 512])])
    add_piece("V01", 8, 512, [(0, 512, wi[:, 3072:3584])])
    add_piece("V2", 8, 256, [(0, 256, wi[:, 3584:3840])])
    add_piece("QM", 8, 512, [(0, 512, wi[:, 3840:4352])])
    tile_order += ["L0", "QK0", "QK1", "L1", "QK2", "V01", "L2", "V2", "QM"]
    wy_src = [w_lru_out.ap(), w_dil_out.ap(), w_mem_out.ap()]
    KB = [6, 2, 4]
    for half in range(2):
        for b in range(3):
            add_piece("Y%d%d" % (b, half), KB[b], 512, [(0, 512, wy_src[b][:, half * 512:(half + 1) * 512])])
            add_piece("G%d%d" % (b, half), 8, 512, [(0, 512, wi[:, 4352 + b * 1024 + half * 512:4352 + b * 1024 + (half + 1) * 512])])
            tile_order += ["Y%d%d" % (b, half), "G%d%d" % (b, half)]
    for half in range(2):
        add_piece("WO%d" % half, 8, 512, [(0, 512, w_out.ap()[:, half * 512:(half + 1) * 512])]); tile_order.append("WO%d" % half)
    for pc in range(8):
        add_piece("MI%d" % pc, 8, 512, [(0, 512, w_mlp_in.ap()[:, pc * 512:(pc + 1) * 512])]); tile_order.append("MI%d" % pc)
    for half in range(2):
        for hg in range(4):
            add_piece("MO%d%d" % (half, hg), 8, 512, [(0, 512, w_mlp_out.ap()[hg * 1024:(hg + 1) * 1024, half * 512:(half + 1) * 512])])
            tile_order.append("MO%d%d" % (half, hg))
    gseq = ["mkv0", "mkv1"] + tile_order * ntiles
    wst = {"issued": 0, "next": 0}

    seen = set()

    def WGET(expect):
        idx = wst["next"]
        assert gseq[idx] == expect, (gseq[idx], expect)
        wst["next"] += 1
        while wst["issued"] < len(gseq) and wst["issued"] <= idx + NSLOT - 2:
            j = wst["issued"]
            name = gseq[j]
            h, K, cols, R, srcs = pieces[name]
            slot = wslots[j % NSLOT]
            if name not in seen:
                seen.add(name)
                sv = slot.t[:, 0:K * cols].rearrange("p (k n) -> p k n", k=K)
                DMA("pool", [(sv[:, :, o:o + n], kp(src)) for (o, n, src) in srcs], [], [slot])
                if ntiles > 1 and not name.startswith("mkv"):
                    DMA("sp", [(h.ap(), slot.t[:, 0:K * cols])], [slot], [R])
            else:
                DMA("sp", [(slot.t[:, 0:K * cols], h.ap())], [R], [slot])
            wst["issued"] += 1
        h, K, cols, R, srcs = pieces[expect]
        slot = wslots[idx % NSLOT]
        return slot.t[:, 0:K * cols].rearrange("p (k n) -> p k n", k=K), slot

    def colvec(dst, src, nchunk):
        DMA("sp", [(dst.t[:, 0:nchunk], src.ap().rearrange("(c p) -> p c", p=128))], [], [dst], slow=True)

    colvec(gmem, g_mem, 8)
    xm = x_tms[1]
    DMA("sp", [(xm.t[:, 0:2, :], mem.ap().rearrange("(s p) d -> p s d", p=128))], [], [xm])
    colvec(gmix, g_mix, 8)
    if ntiles > 0:
        DMA("sp", [(x_tms[0].t[:], x.ap()[0:T, :].rearrange("(s p) d -> p s d", p=128))], [], [x_tms[0]])
    colvec(gmlp, g_mlp, 8); colvec(bgate, b_gate, 24)
    colvec(cb, conv_b, 6); colvec(ba_c, b_rg_a, 6); colvec(bx_c, b_rg_x, 6); colvec(lam, lru_lambda, 6)
    DMA("sp", [(cw.t[:, :, j], conv_w.ap()[j, :].rearrange("(c p) -> p c", p=128)) for j in range(4)], [], [cw], slow=True)
    DMA("sp", [(gfin.t[:], bass.AP(g_final, 0, [[0, 128], [1, D]]))], [], [gfin])
    ACT(c8.t[:], lam.t[:], AF.Exp, [lam], [c8], scale=-1.0)
    ACT(c8.t[:], c8.t[:], AF.Ln, [c8], [c8], bias=1.0)
    TS("dve", c4.t[:], c8.t[:], -4.0, None, ALU.mult, ALU.bypass, [c8], [c4])
    TS("dve", c8.t[:], c8.t[:], -8.0, None, ALU.mult, ALU.bypass, [c8], [c8])
    TS("dve", hba.t[:], ba_c.t[:], 0.5, None, ALU.mult, ALU.bypass, [ba_c], [hba])
    TS("dve", hbx.t[:], bx_c.t[:], 0.5, None, ALU.mult, ALU.bypass, [bx_c], [hbx])
    for (wsrc, dstb) in ((w_rg_a, bda), (w_rg_x, bdx)):
        v = wsrc.ap().rearrange("(c two) i j -> two i c j", two=2)
        DMA("sp", [(bdf.t[0:64, :, 0:64], v[0]), (bdf.t[64:128, :, 64:128], v[1])], [], [bdf])
        CP("dve", dstb.t[:], bdf.t[:], [bdf], [dstb])

    def norm_A(src, nsub):
        for s in range(nsub):
            ACT(xn.t[:, s, :], src.t[:, s, :], AF.Square, [src.R[s] if len(src.R) == 4 else src], [xn, ss], accum=ss.t[:, s:s + 1])
        TS("dve", rstd.t[:, 0:nsub], ss.t[:, 0:nsub], 1.0 / D, 1e-6, ALU.mult, ALU.add, [ss], [rstd])
        ACT(rstd.t[:, 0:nsub], rstd.t[:, 0:nsub], AF.Sqrt, [rstd], [rstd])
        S.op("dve", lambda e: e.reciprocal(out=rstd.t[:, 0:nsub], in_=rstd.t[:, 0:nsub]), [rstd], [rstd])
        for s in range(nsub):
            if s % 2 == 0:
                ACT(xn.t[:, s, :], src.t[:, s, :], AF.Copy, [src, rstd], [xn], scale=rstd.t[:, s:s + 1])
            else:
                TS("dve", xn.t[:, s, :], src.t[:, s, :], rstd.t[:, s:s + 1], None, ALU.mult, ALU.bypass, [src, rstd], [xn])

    def norm_B(nsub, gcol):
        for c in range(8):
            b = PN()
            pst = banks[b][:].bitcast(BF16)
            for s in range(nsub):
                S.op("pe", lambda e, s=s, c=c, pst=pst: e.transpose(out=pst[:, s * 128:(s + 1) * 128], in_=xn.t[:, s, c * 128:(c + 1) * 128],
                                                                    identity=ident.t[:]), [xn, ident], [R_b[b]])
            n = nsub * 128
            if c % 2 == 0:
                ACT(hT.t[:, c, 0:n], pst[:, 0:n], AF.Copy, [R_b[b], gcol], [hT], scale=gcol.t[:, c:c + 1])
            else:
                TS("dve", hT.t[:, c, 0:n], pst[:, 0:n], gcol.t[:, c:c + 1], None, ALU.mult, ALU.bypass, [R_b[b], gcol], [hT])

    def norm_T(src, nsub, gcol):
        norm_A(src, nsub)
        norm_B(nsub, gcol)

    def proj(lw, Rw, K=8, rhs=None, Rrhs=None, n=T):
        if rhs is None:
            rhs = lambda k: hT.t[:, k, :]
            Rrhs = hT
        b = PN()
        for k in range(K):
            MM(b, banks[b][:, 0:n], lw(k), rhs(k), k == 0, k == K - 1, [Rw, Rrhs])
        return b

    def load_x(tt):
        DMA("sp", [(x_tms[tt % 2].t[:], x.ap()[tt * T:(tt + 1) * T, :].rearrange("(s p) d -> p s d", p=128))], [], [x_tms[tt % 2]])

    DMA("sp", [(tab.t[:], rel_bias.ap())], [], [tab])
    DMA("sp", [(ohsb.t[:], c_oh.ap().rearrange("g b m -> b g m"))], [], [ohsb])
    DMA("sp", [(maskb.t[:], bass.AP(c_mask, 0, [[0, 4], [256, 2], [1, 256]]))], [], [maskb])

    frow = nc.dram_tensor("frow", [6, 4, 256], F32)
    R_frow = Res("frow")
    for gp in range(6):
        g, part = gp // 2, gp % 2
        b = PN()
        MM(b, banks[b][0:4, 0:256], tab.t[:, g * 4:(g + 1) * 4], ohsb.t[:, gp, :], True, True, [tab, ohsb])
        TT("dve", fsb.t[:, gp, :], banks[b][0:4, 0:256], maskb.t[:, part, :], ALU.add, [R_b[b], maskb], [fsb])
    norm_T(xm, 2, gmem)
    w0, s0 = WGET("mkv0")
    for h in range(4):
        b = proj(lambda k, h=h: w0[:, k, h * 128:(h + 1) * 128], s0, rhs=lambda k: hT.t[:, k, 0:256], Rrhs=hT, n=256)
        CP("dve", kmT.t[:, h, :], banks[b][:, 0:256], [R_b[b]], [kmT])
    w1, s1 = WGET("mkv1")
    for kc in range(2):
        b = proj(lambda k, kc=kc: hT.t[:, k, kc * 128:(kc + 1) * 128], hT, rhs=lambda k: w1[:, k, :], Rrhs=s1)
        CP("dve", vm.t[:, kc, :], banks[b][:], [R_b[b]], [vm])

    DMA("sp", [(frow.ap().rearrange("g h m -> h g m"), fsb.t[:])], [fsb], [R_frow])
    frep = nc.dram_tensor("frep", [24, 128, 256], BF16)
    R_frep = Res("frep")
    DMA("pool", [(frep.ap(), bass.AP(frow, 0, [[256, 24], [0, 128], [1, 256]]))], [R_frow], [R_frep])
    pairs = []
    for g in range(3):
        for h in range(4):
            for part in range(2):
                gp = g * 2 + part
                off = ((gp * 4 + h) * 128) * 256 + 127
                pairs.append((B_all.t[:, (g * 4 + h) * 2 + part, :], bass.AP(frep, off, [[255, 128], [1, 128]])))
    DMA("sp", pairs, [R_frep], [B_all])

    def dump(name, buf, shape, dt=F32):
        if name in dbg and name not in dbg_out:
            h = nc.dram_tensor("dbg_" + name, list(shape), dt, kind="ExternalOutput")
            dbg_out[name] = h
            DMA("sp", [(h.ap(), buf.t[:])], [buf], [Res("dbg_" + name)])
            S.wait_for("sp", [Res("dbg_" + name)])

    dump("B_all", B_all, [128, 24, 128])
    dump("kmT", kmT, [128, 4, 256], BF16)
    dump("vm", vm, [128, 2, 512], BF16)

    if ntiles > 0:
        S.phase = 'norm1'
        norm_T(x_tms[0], 4, gmix)
    pending = []
    for t in range(ntiles):
        par = t % 2
        n2, a2 = t // 4, t % 4
        x_tm = x_tms[t % 2]
        if t == 0:
            dump("hT", hT, [128, 8, T], BF16)
        if stop == "hT":
            break

        KT = [kT0, kT1, kT2]
        p0 = 32 * a2

        def lru_s1(c, ci, w, sl):
            XL, XC, XCB, GL = xl2[c % 2], xc2[c % 2], xcb2[c % 2], gl2[c % 2]
            bx = proj(lambda k: w[:, k, ci * 128:(ci + 1) * 128], sl)
            bg = proj(lambda k: w[:, k, (2 + ci) * 128:(3 + ci) * 128], sl)
            CP("dve", XL.t[:, 0:3], carry.t[:, c, :], [carry], [XL])
            CP("dve", XL.t[:, 3:515], banks[bx][:], [R_b[bx]], [XL])
            CP("dve", carry.t[:, c, :], XL.t[:, 512:515], [XL], [carry])
            TS("dve", XC.t[:], XL.t[:, 0:512], cw.t[:, c, 0:1], cb.t[:, c:c + 1], ALU.mult, ALU.add, [XL, cw, cb], [XC])
            for j in range(1, 4):
                STT(XC.t[:], XL.t[:, j:j + 512], cw.t[:, c, j:j + 1], XC.t[:], ALU.mult, ALU.add, [XL, cw, XC], [XC])
            ACT(XCB.t[:], XC.t[:], AF.Copy, [XC], [XCB])
            ACT(GL.t[:], banks[bg][:], AF.Gelu_apprx_tanh, [R_b[bg]], [GL])

        def lru_s2(cs):
            bks = {}
            for c in cs:
                XCB = xcb2[c % 2]
                ba = PN()
                MM(ba, banks[ba][:], bda.t[:, c, :], XCB.t[:], True, True, [bda, XCB])
                bi = PN()
                MM(bi, banks[bi][:], bdx.t[:, c, :], XCB.t[:], True, True, [bdx, XCB])
                bks[c] = (ba, bi)
            for c in cs:
                ba, bi = bks[c]
                ACT(thr2[c % 2].t[:], banks[ba][:], AF.Tanh, [R_b[ba], hba], [thr2[c % 2]], scale=0.5, bias=hba.t[:, c:c + 1])
                ACT(thi2[c % 2].t[:], banks[bi][:], AF.Tanh, [R_b[bi], hbx], [thi2[c % 2]], scale=0.5, bias=hbx.t[:, c:c + 1])
            for c in cs:
                TH, M = thr2[c % 2], m2[c % 2]
                ACT(M.t[:], TH.t[:], AF.Exp, [TH, c8], [M], scale=c8.t[:, c:c + 1], bias=c8.t[:, c:c + 1])
                ACT(TH.t[:], TH.t[:], AF.Exp, [TH, c4], [TH], scale=c4.t[:, c:c + 1], bias=c4.t[:, c:c + 1])
            for c in cs:
                M = m2[c % 2]
                ACT(M.t[:], M.t[:], AF.Sqrt, [M], [M], scale=-1.0, bias=1.0)
            for c in cs:
                XC, GL, TH, TI, M = xc2[c % 2], gl2[c % 2], thr2[c % 2], thi2[c % 2], m2[c % 2]
                STT(TI.t[:], TI.t[:], 1.0, XC.t[:], ALU.add, ALU.mult, [TI, XC], [TI])
                STT(TI.t[:], TI.t[:], 0.5, M.t[:], ALU.mult, ALU.mult, [TI, M], [TI])
                S.op("dve", lambda e, c=c, TH=TH, TI=TI: e.tensor_tensor_scan(out=hl.t[:], data0=TH.t[:], data1=TI.t[:],
                                                                              initial=hstate.t[:, c:c + 1], op0=ALU.mult, op1=ALU.add),
                     [TH, TI, hstate], [hl])
                CP("dve", hstate.t[:, c:c + 1], hl.t[:, 511:512], [hl], [hstate])
                TT("pool", gated.t[:, c, :], GL.t[:], hl.t[:], ALU.mult, [GL, hl], [gated])

        def qk_task(pc, ci):
            def f(w, sl):
                cc = pc * 4 + ci
                b = proj(lambda k: w[:, k, ci * 128:(ci + 1) * 128], sl)
                if cc < 6:
                    ACT(qT.t[:, cc, :], banks[b][:], AF.Copy, [R_b[b]], [qT], scale=0.125)
                else:
                    g, pair = (cc - 6) // 2, (cc - 6) % 2
                    base = (t * T) if g == 2 else par * T
                    ACT(KT[g].t[:, pair, base:base + T], banks[b][:], AF.Copy, [R_b[b]], [KT[g]])
            return f

        def v0_task(j):
            def f(w, sl):
                b = proj(lambda k: hT.t[:, k, j * 128:(j + 1) * 128], hT, rhs=lambda k: w[:, k, 0:256], Rrhs=sl, n=256)
                ACT(V0.t[:, (4 * t + j) % 8, :], banks[b][:, 0:256], AF.Copy, [R_b[b]], [V0])
            return f

        def v1_task(r):
            def f(w, sl):
                b = proj(lambda k: hT.t[:, k, :].rearrange("p (i r) -> p r i", r=4)[:, r, :], hT, rhs=lambda k: w[:, k, 256:512], Rrhs=sl, n=256)
                ACT(V1.t[:, par * 4 + r, :], banks[b][:, 0:256], AF.Copy, [R_b[b]], [V1])
            return f

        def v2_task(j):
            def f(w, sl):
                b = PN()
                for k in range(8):
                    lhs = hT.t[:, k, :].rearrange("p (m f) -> p f m", f=4)[:, j, :]
                    MM(b, banks[b][:, 0:256], lhs, w[:, k, :], k == 0, k == 7, [hT, sl])
                vt = V2tmp[j]
                va = vt.t[:]
                ACT(va, banks[b][:, 0:256], AF.Copy, [R_b[b]], [vt])
                ps_ = va.ap[0][0]
                DMA("act", [(V2.t[p0:p0 + 32, n2 * 16 + 4 * rl + j, :], bass.AP(va.tensor, va.offset + rl * ps_, [[4 * ps_, 32], [1, 256]]))
                           for rl in range(4)], [vt], [V2])
            return f

        fillers = [("QK%d" % pc, qk_task(pc, ci)) for pc in range(3) for ci in range(4)]
        fillers += [("V01", v0_task(j)) for j in range(4)] + [("V01", v1_task(r)) for r in range(4)]
        fillers += [("V2", v2_task(j)) for j in range(4)]
        nfill = [4, 4, 4, 4, 4, 4]
        steps = []
        fi = 0
        for cp in range(3):
            steps.append(("L%d" % cp, lambda w, sl, cp=cp: (lru_s1(cp * 2, 0, w, sl), lru_s1(cp * 2 + 1, 1, w, sl))))
            for ci in range(2):
                c = cp * 2 + ci
                for _ in range(nfill[c]):
                    steps.append(fillers[fi])
                    fi += 1
                if ci == 0:
                    steps.append((None, lambda w, sl, cp=cp: lru_s2((cp * 2, cp * 2 + 1))))
        assert fi == len(fillers)
        S.phase = 'lru'
        fetched = {}
        for (pname, fn) in steps:
            if pname is None:
                fn(None, None)
                continue
            if pname not in fetched:
                fetched[pname] = WGET(pname)
            fn(*fetched[pname])
        if t == 0:
            dump("gated", gated, [128, 6, T], BF16)
        if stop == "gated":
            break
        S.phase = 'mem'
        w, sl = WGET("QM")
        for h in range(4):
            b = proj(lambda k, h=h: w[:, k, h * 128:(h + 1) * 128], sl)
            CP("dve", qmT.t[:, h, :], banks[b][:], [R_b[b]], [qmT])
        if t == 0:
            dump("omT", omT, [128, 4, T], BF16)
        if stop == "omT":
            break

        S.phase = 'dil'
        VV = [V0, V1, V2]

        def q_ap(g, pair, s_, rows):
            q = qT.t[rows, g * 2 + pair, :]
            if g == 0:
                return q[:, s_ * 128:(s_ + 1) * 128]
            return q.rearrange("p (i r) -> p r i", r=DILS[g])[:, s_, :]

        def k_ap(g, pair, part, s_, rows):
            if g == 0:
                col0 = ((4 * t + s_ - (1 - part)) * 128) % 1024
                return kT0.t[rows, pair, col0:col0 + 128]
            if g == 1:
                base = ((t - (1 - part)) % 2) * T
                return kT1.t[rows, pair, base:base + T].rearrange("p (i r) -> p r i", r=4)[:, s_, :]
            base = (n2 - (1 - part)) * 2048
            return kT2.t[rows, pair, base:base + 2048].rearrange("p (i r) -> p r i", r=16)[:, s_, :]

        def v_ap(g, part, s_, h):
            if g == 0:
                blk = (4 * t + s_ - (1 - part)) % 8
            elif g == 1:
                blk = ((t - (1 - part)) % 2) * 4 + s_
            else:
                blk = (n2 - (1 - part)) * 16 + s_
            return VV[g].t[:, blk, h * 64:(h + 1) * 64]

        def has_prev(g, s_):
            if g == 0:
                return 4 * t + s_ > 0
            if g == 1:
                return t > 0
            return n2 > 0

        units = [(pair, g, hh) for pair in range(2) for g in range(3) for hh in range(2)]
        ndb = {}

        def geom(g):
            nb, nq = (4, 128) if g < 2 else (16, 32)
            return nb, nq, (0 if g < 2 else 32 * a2)

        def emit_st(ui):
            pair, g, hh = units[ui]
            nb, nq, qoff = geom(g)
            h = pair * 2 + hh
            rows = slice(hh * 64, hh * 64 + 64)
            for part in range(2):
                sls = [s_ for s_ in range(nb) if part == 1 or has_prev(g, s_)]
                if not sls:
                    continue
                bs = PNS("st")
                for s_ in range(nb):
                    pp = part if s_ in sls else 1
                    MM(bs, banks[bs][:, s_ * nq:(s_ + 1) * nq], k_ap(g, pair, pp, s_, rows), q_ap(g, pair, s_, rows), s_ == 0, False,
                       [KT[g], qT])
                idx = (g * 4 + h) * 2 + part
                bsl = B_all.t[:, idx, qoff:qoff + nq]
                in1 = bass.AP(B_all.t, bsl.offset, [list(bsl.ap[0]), [0, nb], [1, nq]])
                MM(bs, banks[bs][:].rearrange("p (a n) -> p a n", a=nb), ident.t[:], in1, False, True, [ident, B_all])
                ACT(PTs[part][ui % 3].t[:], banks[bs][:], AF.Exp, [R_b[bs]], [PTs[part][ui % 3]])

        def emit_pv(ui):
            pair, g, hh = units[ui]
            nb, nq, qoff = geom(g)
            h = pair * 2 + hh
            orow = slice(hh * 64, hh * 64 + 64)
            if hh == 0:
                ndb[(pair, g)] = (PNS("nd"), PNS("nd"))
            bnum, bden = ndb[(pair, g)]
            for s_ in range(nb):
                parts = [0, 1] if has_prev(g, s_) else [1]
                for ii, part in enumerate(parts):
                    pt = PTs[part][ui % 3]
                    MM(bnum, banks[bnum][orow, s_ * nq:(s_ + 1) * nq], v_ap(g, part, s_, h), pt.t[:, s_ * nq:(s_ + 1) * nq],
                       ii == 0, ii == len(parts) - 1, [VV[g], pt])
            prev_s = [s_ for s_ in range(nb) if has_prev(g, s_)]
            ptc = PTs[1][ui % 3]
            MM(bden, banks[bden][orow, 0:nb * nq], ones.t[:, 0:64], ptc.t[:, 0:nb * nq], True, not prev_s, [ones, ptc])
            if prev_s:
                c0, c1 = prev_s[0] * nq, (prev_s[-1] + 1) * nq
                ptp = PTs[0][ui % 3]
                MM(bden, banks[bden][orow, c0:c1], ones.t[:, 0:64], ptp.t[:, c0:c1], False, True, [ones, ptp])
            if hh == 1:
                if g == 0:
                    ACT(NUM.t[:], banks[bnum][:], AF.Copy, [R_b[bnum]], [NUM])
                    ACT(DEN.t[:], banks[bden][:], AF.Copy, [R_b[bden]], [DEN])
                else:
                    rr = DILS[g]
                    for (dst, bk) in ((NUM, bnum), (DEN, bden)):
                        dv = dst.t[:].rearrange("p (i r) -> p i r", r=rr)
                        TT("dve", dv, dv, banks[bk][:].rearrange("p (r i) -> p i r", r=rr), ALU.add, [dst, R_b[bk]], [dst])
                if g == 2:
                    ACT(rscr.t[:], DEN.t[:], AF.Ln, [DEN], [rscr])
                    ACT(rden.t[:], rscr.t[:], AF.Exp, [rscr], [rden], scale=-1.0)
                    TT("dve", odT.t[:, pair, :], NUM.t[:], rden.t[:], ALU.mult, [NUM, rden], [odT])

        def mem_st(h):
            for kc in range(2):
                bs = PNS("st")
                MM(bs, banks[bs][:], kmT.t[:, h, kc * 128:(kc + 1) * 128], qmT.t[:, h, :], True, True, [kmT, qmT])
                ACT(pTm[kc].t[:], banks[bs][:], AF.Exp, [R_b[bs]], [pTm[kc]], scale=128 ** -0.5)

        def mem_pv(h):
            bo = PNS("st")
            for kc in range(2):
                MM(bo, banks[bo][:], vm.t[:, kc, h * 128:(h + 1) * 128], pTm[kc].t[:], kc == 0, kc == 1, [vm, pTm[kc]])
            bd = PNS("st")
            for kc in range(2):
                MM(bd, banks[bd][:], ones.t[:], pTm[kc].t[:], kc == 0, kc == 1, [ones, pTm[kc]])
            ACT(rscr.t[:], banks[bd][:], AF.Ln, [R_b[bd]], [rscr])
            ACT(rdm.t[:], rscr.t[:], AF.Exp, [rscr], [rdm], scale=-1.0)
            TT("dve", omT.t[:, h, :], banks[bo][:], rdm.t[:], ALU.mult, [R_b[bo], rdm], [omT])

        seq = []
        mh = 0
        for ui in range(len(units)):
            seq.append(("d", ui))
            if ui % 3 == 2:
                seq.append(("m", mh))
                mh += 1

        def do_st(it):
            emit_st(it[1]) if it[0] == "d" else mem_st(it[1])

        def do_pv(it):
            emit_pv(it[1]) if it[0] == "d" else mem_pv(it[1])

        do_st(seq[0])
        do_st(seq[1])
        for si in range(len(seq)):
            if si + 2 < len(seq):
                do_st(seq[si + 2])
            do_pv(seq[si])
        if t == 0:
            dump("odT", odT, [128, 2, T], BF16)
        if stop == "odT":
            break

        S.phase = 'merge'
        opnd = [gated, odT, omT]
        for half in range(2):
            for b3 in range(3):
                wy, sy = WGET("Y%d%d" % (b3, half))
                wg, sgs = WGET("G%d%d" % (b3, half))
                for ci in range(4):
                    c = half * 4 + ci
                    by = proj(lambda k, ci=ci: wy[:, k, ci * 128:(ci + 1) * 128], sy, K=KB[b3],
                              rhs=lambda k, b3=b3: opnd[b3].t[:, k, :], Rrhs=opnd[b3])
                    bgk = proj(lambda k, ci=ci: wg[:, k, ci * 128:(ci + 1) * 128], sgs)
                    sg, tmpm = sgbufs[ci % 2], tmpbufs[ci % 2]
                    accR = R_U[22 + 2 * ci:24 + 2 * ci]
                    mrgR = R_U[30 + c]
                    ACT(sg.t[:], banks[bgk][:], AF.Sigmoid, [R_b[bgk], bgate], [sg], bias=bgate.t[:, b3 * 8 + c:b3 * 8 + c + 1])
                    if b3 == 0:
                        TT("dve", acc.t[:, ci, :], sg.t[:], banks[by][:], ALU.mult, [sg, R_b[by]], [accR])
                    else:
                        TT("dve", tmpm.t[:], sg.t[:], banks[by][:], ALU.mult, [sg, R_b[by]], [tmpm])
                        if b3 == 1:
                            TT("pool", acc.t[:, ci, :], acc.t[:, ci, :], tmpm.t[:], ALU.add, [accR, tmpm], [accR])
                        else:
                            TT("pool", mergedbf.t[:, c, :], acc.t[:, ci, :], tmpm.t[:], ALU.add, [accR, tmpm], [mrgR])
        if t == 0:
            dump("mergedbf", mergedbf, [128, 8, T], BF16)
        if stop == "mergedbf":
            break
        S.phase = 'wout'
        if t + 1 < ntiles:
            load_x(t + 1)
        wo = [WGET("WO0"), WGET("WO1")]
        for s in range(4):
            for half in range(2):
                w, sl = wo[half]
                b = proj(lambda k, s=s: mergedbf.t[:, k, s * 128:(s + 1) * 128], mergedbf, rhs=lambda k: w[:, k, :], Rrhs=sl)
                xs = x_tm.t[:, s, half * 512:(half + 1) * 512]
                TT("dve", xs, xs, banks[b][:], ALU.add, [x_tm.R[s], R_b[b]], [x_tm.R[s]])
        if t == 0:
            dump("x1", x_tm, [128, 4, D])
        if stop == "x1":
            break
        S.phase = 'mlpin'
        norm_T(x_tm, 4, gmlp)
        if t + 1 < ntiles:
            norm_A(x_tms[(t + 1) % 2], 4)
        for pc in range(8):
            w, sl = WGET("MI%d" % pc)
            for ci in range(4):
                hc = pc * 4 + ci
                b = proj(lambda k, ci=ci: w[:, k, ci * 128:(ci + 1) * 128], sl)
                rb = rl[hc % 2]
                ACT(rb.t[:], banks[b][:], AF.Relu, [R_b[b]], [rb])
                TT("pool" if hc % 2 else "dve", uT[hc].t[:], rb.t[:], rb.t[:], ALU.mult, [rb], [uT[hc]])
        if stop == "mlpin":
            break
        if t + 1 < ntiles:
            S.phase = 'norm1'
            norm_B(4, gmix)
        S.phase = 'mlpout'
        for half in range(2):
            accs = [PN() for _ in range(4)]
            for hg in range(4):
                w, sl = WGET("MO%d%d" % (half, hg))
                for hcl in range(8):
                    hc = hg * 8 + hcl
                    for s in range(4):
                        MM(accs[s], banks[accs[s]][:], uT[hc].t[:, s * 128:(s + 1) * 128], w[:, hcl, :], hc == 0, hc == 31, [uT[hc], sl])
            for s in range(4):
                xs = x_tm.t[:, s, half * 512:(half + 1) * 512]
                TT("dve", xs, xs, banks[accs[s]][:], ALU.add, [x_tm, R_b[accs[s]]], [x_tm])
        if stop == "mlpout":
            dump("x2", x_tm, [128, 4, D])
            break
        def final_parts(t=t, x_tm=x_tm):
            def sq(s):
                return lambda: ACT(xn.t[:, s, :], x_tm.t[:, s, :], AF.Square, [x_tm.R[s]], [xn, ss], accum=ss.t[:, s:s + 1])

            def rest():
                TS("dve", rstd.t[:], ss.t[:], 1.0 / D, 1e-6, ALU.mult, ALU.add, [ss], [rstd])
                ACT(rstd.t[:], rstd.t[:], AF.Sqrt, [rstd], [rstd])
                S.op("dve", lambda e: e.reciprocal(out=rstd.t[:], in_=rstd.t[:]), [rstd], [rstd])
                for s in range(4):
                    STT(x_tm.t[:, s, :], x_tm.t[:, s, :], rstd.t[:, s:s + 1], gfin.t[:], ALU.mult, ALU.mult, [x_tm.R[s], rstd, gfin], [x_tm.R[s]])
                DMA("sp", [(y.ap()[t * T:(t + 1) * T, :].rearrange("(s p) d -> p s d", p=128), x_tm.t[:])], [x_tm], [R_y])
            return [sq(0), sq(1), sq(2), sq(3), rest]

        S.phase = 'final'
        for fn in final_parts():
            fn()

    for fn in pending:
        fn()
    pending.clear()
    S.wait_for("sp", [R_y])
    S.emit()
    global _LAST_SCHED
    _LAST_SCHED = S
    return nc, dbg_out


_LAST_SCHED = None


def kernel(**inputs):
    names = ["g_mix", "w_in", "b_gate", "conv_w", "conv_b", "w_rg_a", "b_rg_a", "w_rg_x", "b_rg_x", "lru_lambda", "w_lru_out",
             "rel_bias", "w_dil_out", "g_mem", "w_mem_kv", "w_mem_out", "w_out", "g_mlp", "w_mlp_in", "w_mlp_out", "g_final"]
    oh, mask = _host_consts()
    shared = {n: np.ascontiguousarray(np.asarray(inputs[n], dtype=np.float32)) for n in names}
    shared["c_oh"] = oh
    shared["c_mask"] = mask
    xs = np.asarray(inputs["x"], dtype=np.float32)
    mems = np.asarray(inputs["mem"], dtype=np.float32)
    nb = xs.shape[0]
    in_maps = []
    for b in range(nb):
        m = dict(shared)
        m["x"] = np.ascontiguousarray(xs[b])
        m["mem"] = np.ascontiguousarray(mems[b])
        in_maps.append(m)
    nc, _ = build_program()
    res = run_bass_kernel_spmd(nc, in_maps, core_ids=list(range(nb)))
    return np.stack([np.asarray(r["y"]) for r in res.results], axis=0).astype(np.float32)
```

```python
import math
import numpy as np
import concourse.bass as bass
import concourse.mybir as mybir
from concourse.bass_utils import run_bass_kernel_spmd

F32 = mybir.dt.float32
BF16 = mybir.dt.bfloat16
AF = mybir.ActivationFunctionType
ALU = mybir.AluOpType

D = 1024
SEQ = 4096
T = 512
NT = SEQ // T
NEG = -1e30
DILS = (1, 4, 16)
SBUF_BYTES = 229376
U_KB = 44
UOFF = SBUF_BYTES - (U_KB + 8) * 1024


class Res:
    __slots__ = ("name", "w", "r", "dsem", "dcnt", "dsem2", "dcnt2")

    def __init__(self, name):
        self.name = name
        self.w = None
        self.r = []
        self.dsem = None
        self.dcnt = 0
        self.dsem2 = None
        self.dcnt2 = 0


class Buf:
    def __init__(self, t, R):
        self.t = t
        self.R = R


def _flat(xs):
    out = []
    for x in xs:
        if isinstance(x, Res):
            out.append(x)
        elif isinstance(x, Buf):
            out.extend(x.R)
        else:
            out.extend(_flat(x))
    return out


class Sched:
    ENG = ("pe", "act", "dve", "pool", "sp")
    SAME_ENG_SYNC = ("act", "dve", "pool")

    def __init__(self, nc):
        self.nc = nc
        self.streams = {e: [] for e in self.ENG}
        self.count = {e: 0 for e in self.ENG}
        self.waited = {e: {} for e in self.ENG}
        self.sem = {}
        for e in self.ENG:
            self.sem["e_" + e] = nc.alloc_semaphore("sem_" + e)
        self.ndsem = 0
        self.phase = 'setup'
        self.tags = {e: [] for e in self.ENG}

    def _collect(self, eng, reads, writes):
        deps = {}

        def need(ev):
            if ev is None:
                return
            s, v = ev
            if deps.get(s, 0) < v:
                deps[s] = v

        for r in reads:
            need(r.w)
        for w in writes:
            need(w.w)
            for ev in w.r:
                need(ev)
        waits = []
        for s, v in deps.items():
            if s == "e_" + eng and eng not in self.SAME_ENG_SYNC:
                continue
            if self.waited[eng].get(s, 0) >= v:
                continue
            self.waited[eng][s] = v
            waits.append((s, v))
        return waits

    def _post(self, ev, reads, writes):
        for r in reads:
            r.r.append(ev)
            if len(r.r) > 12:
                best = {}
                for s, v in r.r:
                    if best.get(s, 0) < v:
                        best[s] = v
                r.r = list(best.items())
        for w in writes:
            w.w = ev
            w.r = []

    def op(self, eng, fn, reads=(), writes=()):
        reads = _flat(reads)
        writes = _flat(writes)
        waits = self._collect(eng, reads, writes)
        self.count[eng] += 1
        ev = ("e_" + eng, self.count[eng])
        self.streams[eng].append((waits, fn, ev[0], 1))
        self.tags[eng].append(self.phase)
        self._post(ev, reads, writes)

    def dma(self, q, fns, reads=(), writes=()):
        reads = _flat(reads)
        writes = _flat(writes)
        dst = writes[0]
        sw = (q == "pool")
        if (dst.dsem2 if sw else dst.dsem) is None:
            nm = "d_%d" % self.ndsem
            self.sem[nm] = self.nc.alloc_semaphore("dsem%d" % self.ndsem)
            self.ndsem += 1
            if sw:
                dst.dsem2 = nm
            else:
                dst.dsem = nm
        waits = self._collect(q, reads, writes)
        if sw:
            dst.dcnt2 += 16 * len(fns)
            ev = (dst.dsem2, dst.dcnt2)
        else:
            dst.dcnt += 16 * len(fns)
            ev = (dst.dsem, dst.dcnt)
        for i, fn in enumerate(fns):
            self.streams[q].append((waits if i == 0 else [], fn, ev[0], 16))
        self._post(ev, reads, writes)

    def wait_for(self, eng, ress):
        waits = self._collect(eng, _flat(ress), ())
        self.streams[eng].append((waits, None, None, 0))

    def emit(self):
        names = {"pe": "tensor", "act": "scalar", "dve": "vector", "pool": "gpsimd", "sp": "sync"}
        with self.nc.Block() as block:
            for eng in self.ENG:
                def f(e, eng=eng):
                    for waits, fn, semname, inc in self.streams[eng]:
                        for (s, v) in waits:
                            e.wait_ge(self.sem[s], v)
                        if fn is not None:
                            fn(e).then_inc(self.sem[semname], inc)
                getattr(block, names[eng])(f)


def _t5_bucket(dist):
    dist = np.asarray(dist, dtype=np.int64)
    df = np.maximum(dist, 1).astype(np.float32)
    val = np.log(df / np.float32(16.0)) / np.float32(math.log(2048 / 16)) * np.float32(16.0)
    large = 16 + val.astype(np.float32).astype(np.int32)
    large = np.minimum(large, 31)
    return np.where(dist < 16, dist, large)


def _host_consts():
    oh = np.zeros((6, 32, 256), np.float32)
    mask = np.zeros((2, 256), np.float32)
    for g, dil in enumerate(DILS):
        for part in range(2):
            for m in range(255):
                d = m - 127
                if part == 0:
                    valid, off = d <= 0, d + 128
                else:
                    valid, off = d >= 0, d
                if valid:
                    oh[g * 2 + part, int(_t5_bucket(off * dil)), m] = 1.0
                else:
                    mask[part, m] = NEG
    return oh, mask


def build_program(ntiles=NT, dbg=(), stop=None):
    nc = bass.Bass("TRN2", target_bir_lowering=False)
    S = Sched(nc)

    def din(name, shape):
        return nc.dram_tensor(name, list(shape), F32, kind="ExternalInput")

    x = din("x", [SEQ, D]); mem = din("mem", [256, D])
    g_mix = din("g_mix", [D]); w_in = din("w_in", [D, 7424]); b_gate = din("b_gate", [3072])
    conv_w = din("conv_w", [4, 768]); conv_b = din("conv_b", [768])
    w_rg_a = din("w_rg_a", [12, 64, 64]); b_rg_a = din("b_rg_a", [768])
    w_rg_x = din("w_rg_x", [12, 64, 64]); b_rg_x = din("b_rg_x", [768])
    lru_lambda = din("lru_lambda", [768]); w_lru_out = din("w_lru_out", [768, D])
    rel_bias = din("rel_bias", [32, 12]); w_dil_out = din("w_dil_out", [256, D])
    g_mem = din("g_mem", [D]); w_mem_kv = din("w_mem_kv", [D, D]); w_mem_out = din("w_mem_out", [512, D])
    w_out = din("w_out", [D, D]); g_mlp = din("g_mlp", [D]); w_mlp_in = din("w_mlp_in", [D, 4096])
    w_mlp_out = din("w_mlp_out", [4096, D]); g_final = din("g_final", [D])
    c_oh = din("c_oh", [6, 32, 256]); c_mask = din("c_mask", [2, 256])
    y = nc.dram_tensor("y", [SEQ, D], F32, kind="ExternalOutput")
    R_y = Res("y")
    R_ys = [Res("y%d" % i) for i in range(4)]
    dbg_out = {}

    def SB(name, shape, dt):
        return Buf(nc.alloc_sbuf_tensor(name, list(shape), dt), [Res(name)])

    x_tms = [SB("x_tm0", [128, 4, D], F32), SB("x_tm1", [128, 4, D], F32)]
    for _xb in x_tms:
        _xb.R = [Res(_xb.R[0].name + "_s%d" % i) for i in range(4)]
    x_tm = x_tms[0]
    xn = SB("xn", [128, 4, D], BF16)
    ss = SB("ss", [128, 4], F32)
    rstd = SB("rstd", [128, 4], F32)
    hT = SB("hT", [128, 8, T], BF16)
    ident = SB("ident", [128, 128], BF16)
    ones = SB("ones", [128, 128], BF16)
    gmix = SB("gmix", [128, 8], F32); gmem = SB("gmem", [128, 8], F32); gmlp = SB("gmlp", [128, 8], F32)
    bgate = SB("bgate", [128, 24], F32)
    cw = SB("cw", [128, 6, 4], F32); cb = SB("cb", [128, 6], F32)
    ba_c = SB("ba_c", [128, 6], F32); bx_c = SB("bx_c", [128, 6], F32)
    lam = SB("lam", [128, 6], F32); c8 = SB("c8", [128, 6], F32); c4 = SB("c4", [128, 6], F32)
    hba = SB("hba", [128, 6], F32); hbx = SB("hbx", [128, 6], F32)
    gfin = SB("gfin", [128, D], F32)
    bda = SB("bda", [128, 6, 128], BF16); bdx = SB("bdx", [128, 6, 128], BF16)
    carry = SB("carry", [128, 6, 3], F32); hstate = SB("hstate", [128, 6], F32)
    kmT = SB("kmT", [128, 4, 256], BF16); vm = SB("vm", [128, 2, 512], BF16)
    kT0 = SB("kT0", [128, 2, 1024], BF16); kT1 = SB("kT1", [128, 2, 1024], BF16); kT2 = SB("kT2", [128, 2, SEQ], BF16)
    V0 = SB("V0", [128, 8, 256], BF16); V1 = SB("V1", [128, 8, 256], BF16); V2 = SB("V2", [128, 32, 256], BF16)
    B_all = SB("B_all", [128, 24, 128], BF16)
    NSLOT = 5
    wslots = [SB("wslot%d" % i, [128, 4096], BF16) for i in range(NSLOT)]
    assert nc.sbuf_bytes_remaining >= (U_KB + 8) * 1024 + 64, nc.sbuf_bytes_remaining

    R_U = [Res("U%d" % i) for i in range(U_KB)]

    def UB(name, shape, dt, off_kb):
        nbytes = int(np.prod(shape[1:])) * (4 if dt == F32 else 2)
        t = nc.alloc_sbuf_tensor_at(name, list(shape), dt, offset=UOFF + off_kb * 1024)
        return Buf(t, R_U[off_kb: off_kb + (nbytes + 1023) // 1024])

    identf = UB("identf", [128, 128], F32, 0); bdf = UB("bdf", [128, 6, 128], F32, 1)
    ohsb = UB("ohsb", [32, 6, 256], F32, 4); maskb = UB("maskb", [4, 2, 256], F32, 10)
    fsb = UB("fsb", [4, 6, 256], F32, 12); tab = UB("tab", [32, 12], F32, 18)
    gated = UB("gated", [128, 6, T], BF16, 0)
    qT = UB("qT", [128, 6, T], BF16, 6)
    xl2 = [UB("xl0", [128, 515], F32, 12), UB("xl1", [128, 515], F32, 15)]
    xc2 = [UB("xc0", [128, T], F32, 18), UB("xc1", [128, T], F32, 20)]
    xcb2 = [UB("xcb0", [128, T], BF16, 22), UB("xcb1", [128, T], BF16, 23)]
    gl2 = [UB("gl0", [128, T], F32, 24), UB("gl1", [128, T], F32, 26)]
    thr2 = [UB("thr0", [128, T], F32, 28), UB("thr1", [128, T], F32, 30)]
    thi2 = [UB("thi0", [128, T], F32, 32), UB("thi1", [128, T], F32, 34)]
    m2 = [UB("m0", [128, T], F32, 36), UB("m1", [128, T], F32, 38)]
    hl = UB("hl", [128, T], F32, 40)
    V2tmp = [UB("V2tmp%d" % i, [128, 256], BF16, 42) for i in range(1)]
    _v2h = nc.alloc_sbuf_tensor_at("V2tmp_all", [128, 4, 256], BF16, offset=UOFF + 42 * 1024)
    V2tmp = [Buf(_v2h[:, i, :], [R_U[42 + i // 2]]) for i in range(4)]
    qmT = UB("qmT", [128, 4, T], BF16, 12)
    omT = UB("omT", [128, 4, T], BF16, 16)
    odT = UB("odT", [128, 2, T], BF16, 20)
    PTs = [[UB("PT0%d" % i, [128, T], BF16, 22 + i) for i in range(3)],
           [UB("PT1%d" % i, [128, T], BF16, 25 + i) for i in range(3)]]
    NUM = UB("NUM", [128, T], F32, 30); DEN = UB("DEN", [128, T], F32, 32); rden = UB("rden", [128, T], F32, 34)
    pTm = [UB("pTm0", [128, T], BF16, 36), UB("pTm1", [128, T], BF16, 37)]
    rscr = UB("rscr", [128, T], F32, 28)
    rdm = rden
    sgbufs = [UB("sg0", [128, T], F32, 6), UB("sg1", [128, T], F32, 12)]
    tmpbufs = [UB("tmpm0", [128, T], F32, 8), UB("tmpm1", [128, T], F32, 10)]
    acc = UB("acc", [128, 4, T], F32, 22); mergedbf = UB("mergedbf", [128, 8, T], BF16, 30)
    uT = [UB("uT%d" % i, [128, T], BF16, i) for i in range(32)]
    rl = [UB("rl0", [128, T], F32, 32), UB("rl1", [128, T], F32, 34)]

    banks = [nc.alloc_psum_tensor("ps%d" % i, [128, 512], F32) for i in range(8)]
    R_b = [Res("ps%d" % i) for i in range(8)]
    pstate = [0]

    def PN():
        i = pstate[0] % 8
        pstate[0] += 1
        return i

    sub = {"st": [0, (0, 1, 2, 3)], "nd": [0, (4, 5, 6, 7)]}

    def PNS(kind):
        st = sub[kind]
        i = st[1][st[0] % len(st[1])]
        st[0] += 1
        return i

    def ACT(out, in_, func, reads, writes, bias=None, scale=None, accum=None):
        kw = {}
        if bias is not None:
            kw["bias"] = bias
        if scale is not None:
            kw["scale"] = scale
        if accum is not None:
            kw["accum_out"] = accum
        S.op("act", lambda e: e.activation(out=out, in_=in_, func=func, **kw), reads, writes)

    def TS(eng, out, in0, s1, s2, op0, op1, reads, writes):
        S.op(eng, lambda e: e.tensor_scalar(out=out, in0=in0, scalar1=s1, scalar2=s2, op0=op0, op1=op1), reads, writes)

    def TT(eng, out, in0, in1, op, reads, writes):
        S.op(eng, lambda e: e.tensor_tensor(out=out, in0=in0, in1=in1, op=op), reads, writes)

    def STT(out, in0, scalar, in1, op0, op1, reads, writes):
        S.op("dve", lambda e: e.scalar_tensor_tensor(out=out, in0=in0, scalar=scalar, in1=in1, op0=op0, op1=op1), reads, writes)

    def CP(eng, out, in_, reads, writes):
        S.op(eng, lambda e: e.tensor_copy(out, in_), reads, writes)

    def MM(b, out, lhsT, rhs, start, stop, reads):
        S.op("pe", lambda e: e.matmul(out, lhsT=lhsT, rhs=rhs, start=start, stop=stop), reads, [R_b[b]])

    def DMA(q, pairs, reads, writes, slow=False):
        fns = []
        for (o, i) in pairs:
            if slow:
                fns.append(lambda e, o=o, i=i: e.dma_start(out=o, in_=i, allow_slow_non_contiguous=True))
            else:
                fns.append(lambda e, o=o, i=i: e.dma_start(out=o, in_=i))
        S.dma(q, fns, reads, writes)

    S.op("pool", lambda e: e.memset(identf.t[:], 1.0), [], [identf])
    S.op("pool", lambda e: e.affine_select(out=identf.t[:], in_=identf.t[:], pattern=[[-1, 128]], compare_op=ALU.is_equal,
                                           fill=0.0, base=0, channel_multiplier=1), [identf], [identf])
    CP("dve", ident.t[:], identf.t[:], [identf], [ident])
    S.op("dve", lambda e: e.memset(ones.t[:], 1.0), [], [ones])
    for buf in (kT2, V2, carry, hstate, bdf):
        S.op("pool", lambda e, buf=buf: e.memset(buf.t[:], 0.0), [], [buf])

    pieces = {}

    def kp(ap):
        return ap.rearrange("(k p) n -> p k n", p=128)

    def add_piece(name, K, cols, srcs):
        h = nc.dram_tensor("wsc_" + name, [128, K * cols], BF16)
        R = Res("wsc_" + name)
        pieces[name] = (h, K, cols, R, srcs)

    wi = w_in.ap()
    tile_order = []
    add_piece("mkv0", 8, 512, [(0, 512, w_mem_kv.ap()[:, 0:512])])
    add_piece("mkv1", 8, 512, [(0, 512, w_mem_kv.ap()[:, 512:1024])])
    for cp in range(3):
        add_piece("L%d" % cp, 8, 512, [(0, 256, wi[:, cp * 256:(cp + 1) * 256]), (256, 256, wi[:, 768 + cp * 256:768 + (cp + 1) * 256])])
    for pc in range(3):
        add_piece("QK%d" % pc, 8, 512, [(0, 512, wi[:, 1536 + pc * 512:1536 + (pc + 1) * 512])])
    add_piece("V01", 8, 512, [(0, 512, wi[:, 3072:3584])])
    add_piece("V2", 8, 256, [(0, 256, wi[:, 3584:3840])])
    add_piece("QM", 8, 512, [(0, 512, wi[:, 3840:4352])])
    tile_order += ["L0", "QK0", "QK1", "L1", "QK2", "V01", "L2", "V2", "QM"]
    wy_src = [w_lru_out.ap(), w_dil_out.ap(), w_mem_out.ap()]
    KB = [6, 2, 4]
    for half in range(2):
        for b in range(3):
            add_piece("Y%d%d" % (b, half), KB[b], 512, [(0, 512, wy_src[b][:, half * 512:(half + 1) * 512])])
            add_piece("G%d%d" % (b, half), 8, 512, [(0, 512, wi[:, 4352 + b * 1024 + half * 512:4352 + b * 1024 + (half + 1) * 512])])
            tile_order += ["Y%d%d" % (b, half), "G%d%d" % (b, half)]
    for half in range(2):
        add_piece("WO%d" % half, 8, 512, [(0, 512, w_out.ap()[:, half * 512:(half + 1) * 512])]); tile_order.append("WO%d" % half)
    for pc in range(8):
        add_piece("MI%d" % pc, 8, 512, [(0, 512, w_mlp_in.ap()[:, pc * 512:(pc + 1) * 512])]); tile_order.append("MI%d" % pc)
    for half in range(2):
        for hg in range(4):
            add_piece("MO%d%d" % (half, hg), 8, 512, [(0, 512, w_mlp_out.ap()[hg * 1024:(hg + 1) * 1024, half * 512:(half + 1) * 512])])
            tile_order.append("MO%d%d" % (half, hg))
    gseq = ["mkv0", "mkv1"] + tile_order * ntiles
    wst = {"issued": 0, "next": 0}

    seen = set()

    def WGET(expect):
        idx = wst["next"]
        assert gseq[idx] == expect, (gseq[idx], expect)
        wst["next"] += 1
        while wst["issued"] < len(gseq) and wst["issued"] <= idx + NSLOT - 2:
            j = wst["issued"]
            name = gseq[j]
            h, K, cols, R, srcs = pieces[name]
            slot = wslots[j % NSLOT]
            if name not in seen:
                seen.add(name)
                sv = slot.t[:, 0:K * cols].rearrange("p (k n) -> p k n", k=K)
                DMA("pool", [(sv[:, :, o:o + n], kp(src)) for (o, n, src) in srcs], [], [slot])
                if ntiles > 1 and not name.startswith("mkv"):
                    DMA("sp", [(h.ap(), slot.t[:, 0:K * cols])], [slot], [R])
            else:
                DMA("sp", [(slot.t[:, 0:K * cols], h.ap())], [R], [slot])
            wst["issued"] += 1
        h, K, cols, R, srcs = pieces[expect]
        slot = wslots[idx % NSLOT]
        return slot.t[:, 0:K * cols].rearrange("p (k n) -> p k n", k=K), slot

    def colvec(dst, src, nchunk):
        DMA("sp", [(dst.t[:, 0:nchunk], src.ap().rearrange("(c p) -> p c", p=128))], [], [dst], slow=True)

    colvec(gmem, g_mem, 8)
    xm = x_tms[1]
    DMA("sp", [(xm.t[:, 0:2, :], mem.ap().rearrange("(s p) d -> p s d", p=128))], [], [xm])
    colvec(gmix, g_mix, 8)
    if ntiles > 0:
        DMA("sp", [(x_tms[0].t[:], x.ap()[0:T, :].rearrange("(s p) d -> p s d", p=128))], [], [x_tms[0]])
    colvec(gmlp, g_mlp, 8); colvec(bgate, b_gate, 24)
    colvec(cb, conv_b, 6); colvec(ba_c, b_rg_a, 6); colvec(bx_c, b_rg_x, 6); colvec(lam, lru_lambda, 6)
    DMA("sp", [(cw.t[:, :, j], conv_w.ap()[j, :].rearrange("(c p) -> p c", p=128)) for j in range(4)], [], [cw], slow=True)
    DMA("sp", [(gfin.t[:], bass.AP(g_final, 0, [[0, 128], [1, D]]))], [], [gfin])
    ACT(c8.t[:], lam.t[:], AF.Exp, [lam], [c8], scale=-1.0)
    ACT(c8.t[:], c8.t[:], AF.Ln, [c8], [c8], bias=1.0)
    TS("dve", c4.t[:], c8.t[:], -4.0, None, ALU.mult, ALU.bypass, [c8], [c4])
    TS("dve", c8.t[:], c8.t[:], -8.0, None, ALU.mult, ALU.bypass, [c8], [c8])
    TS("dve", hba.t[:], ba_c.t[:], 0.5, None, ALU.mult, ALU.bypass, [ba_c], [hba])
    TS("dve", hbx.t[:], bx_c.t[:], 0.5, None, ALU.mult, ALU.bypass, [bx_c], [hbx])
    for (wsrc, dstb) in ((w_rg_a, bda), (w_rg_x, bdx)):
        v = wsrc.ap().rearrange("(c two) i j -> two i c j", two=2)
        DMA("sp", [(bdf.t[0:64, :, 0:64], v[0]), (bdf.t[64:128, :, 64:128], v[1])], [], [bdf])
        CP("dve", dstb.t[:], bdf.t[:], [bdf], [dstb])

    def norm_A(src, nsub):
        for s in range(nsub):
            ACT(xn.t[:, s, :], src.t[:, s, :], AF.Square, [src.R[s] if len(src.R) == 4 else src], [xn, ss], accum=ss.t[:, s:s + 1])
        TS("dve", rstd.t[:, 0:nsub], ss.t[:, 0:nsub], 1.0 / D, 1e-6, ALU.mult, ALU.add, [ss], [rstd])
        ACT(rstd.t[:, 0:nsub], rstd.t[:, 0:nsub], AF.Sqrt, [rstd], [rstd])
        S.op("dve", lambda e: e.reciprocal(out=rstd.t[:, 0:nsub], in_=rstd.t[:, 0:nsub]), [rstd], [rstd])
        for s in range(nsub):
            if s % 2 == 0:
                ACT(xn.t[:, s, :], src.t[:, s, :], AF.Copy, [src, rstd], [xn], scale=rstd.t[:, s:s + 1])
            else:
                TS("dve", xn.t[:, s, :], src.t[:, s, :], rstd.t[:, s:s + 1], None, ALU.mult, ALU.bypass, [src, rstd], [xn])

    def norm_B(nsub, gcol):
        for c in range(8):
            b = PN()
            pst = banks[b][:].bitcast(BF16)
            for s in range(nsub):
                S.op("pe", lambda e, s=s, c=c, pst=pst: e.transpose(out=pst[:, s * 128:(s + 1) * 128], in_=xn.t[:, s, c * 128:(c + 1) * 128],
                                                                    identity=ident.t[:]), [xn, ident], [R_b[b]])
            n = nsub * 128
            if c % 2 == 0:
                ACT(hT.t[:, c, 0:n], pst[:, 0:n], AF.Copy, [R_b[b], gcol], [hT], scale=gcol.t[:, c:c + 1])
            else:
                TS("dve", hT.t[:, c, 0:n], pst[:, 0:n], gcol.t[:, c:c + 1], None, ALU.mult, ALU.bypass, [R_b[b], gcol], [hT])

    def norm_T(src, nsub, gcol):
        norm_A(src, nsub)
        norm_B(nsub, gcol)

    def proj(lw, Rw, K=8, rhs=None, Rrhs=None, n=T):
        if rhs is None:
            rhs = lambda k: hT.t[:, k, :]
            Rrhs = hT
        b = PN()
        for k in range(K):
            MM(b, banks[b][:, 0:n], lw(k), rhs(k), k == 0, k == K - 1, [Rw, Rrhs])
        return b

    def load_x(tt):
        DMA("sp", [(x_tms[tt % 2].t[:], x.ap()[tt * T:(tt + 1) * T, :].rearrange("(s p) d -> p s d", p=128))], [], [x_tms[tt % 2]])

    DMA("sp", [(tab.t[:], rel_bias.ap())], [], [tab])
    DMA("sp", [(ohsb.t[:], c_oh.ap().rearrange("g b m -> b g m"))], [], [ohsb])
    DMA("sp", [(maskb.t[:], bass.AP(c_mask, 0, [[0, 4], [256, 2], [1, 256]]))], [], [maskb])

    frow = nc.dram_tensor("frow", [6, 4, 256], F32)
    R_frow = Res("frow")
    for gp in range(6):
        g, part = gp // 2, gp % 2
        b = PN()
        MM(b, banks[b][0:4, 0:256], tab.t[:, g * 4:(g + 1) * 4], ohsb.t[:, gp, :], True, True, [tab, ohsb])
        TT("dve", fsb.t[:, gp, :], banks[b][0:4, 0:256], maskb.t[:, part, :], ALU.add, [R_b[b], maskb], [fsb])
    norm_T(xm, 2, gmem)
    w0, s0 = WGET("mkv0")
    for h in range(4):
        b = proj(lambda k, h=h: w0[:, k, h * 128:(h + 1) * 128], s0, rhs=lambda k: hT.t[:, k, 0:256], Rrhs=hT, n=256)
        CP("dve", kmT.t[:, h, :], banks[b][:, 0:256], [R_b[b]], [kmT])
    w1, s1 = WGET("mkv1")
    for kc in range(2):
        b = proj(lambda k, kc=kc: hT.t[:, k, kc * 128:(kc + 1) * 128], hT, rhs=lambda k: w1[:, k, :], Rrhs=s1)
        CP("dve", vm.t[:, kc, :], banks[b][:], [R_b[b]], [vm])

    DMA("sp", [(frow.ap().rearrange("g h m -> h g m"), fsb.t[:])], [fsb], [R_frow])
    frep = nc.dram_tensor("frep", [24, 128, 256], BF16)
    R_frep = Res("frep")
    DMA("pool", [(frep.ap(), bass.AP(frow, 0, [[256, 24], [0, 128], [1, 256]]))], [R_frow], [R_frep])
    pairs = []
    for g in range(3):
        for h in range(4):
            for part in range(2):
                gp = g * 2 + part
                off = ((gp * 4 + h) * 128) * 256 + 127
                pairs.append((B_all.t[:, (g * 4 + h) * 2 + part, :], bass.AP(frep, off, [[255, 128], [1, 128]])))
    DMA("sp", pairs, [R_frep], [B_all])

    def dump(name, buf, shape, dt=F32):
        if name in dbg and name not in dbg_out:
            h = nc.dram_tensor("dbg_" + name, list(shape), dt, kind="ExternalOutput")
            dbg_out[name] = h
            DMA("sp", [(h.ap(), buf.t[:])], [buf], [Res("dbg_" + name)])
            S.wait_for("sp", [Res("dbg_" + name)])

    dump("B_all", B_all, [128, 24, 128])
    dump("kmT", kmT, [128, 4, 256], BF16)
    dump("vm", vm, [128, 2, 512], BF16)

    if ntiles > 0:
        S.phase = 'norm1'
        norm_T(x_tms[0], 4, gmix)
    pending = []
    for t in range(ntiles):
        par = t % 2
        n2, a2 = t // 4, t % 4
        x_tm = x_tms[t % 2]
        if t == 0:
            dump("hT", hT, [128, 8, T], BF16)
        if stop == "hT":
            break

        KT = [kT0, kT1, kT2]
        p0 = 32 * a2

        def lru_s1(c, ci, w, sl):
            XL, XC, XCB, GL = xl2[c % 2], xc2[c % 2], xcb2[c % 2], gl2[c % 2]
            bx = proj(lambda k: w[:, k, ci * 128:(ci + 1) * 128], sl)
            bg = proj(lambda k: w[:, k, (2 + ci) * 128:(3 + ci) * 128], sl)
            CP("dve", XL.t[:, 0:3], carry.t[:, c, :], [carry], [XL])
            CP("dve", XL.t[:, 3:515], banks[bx][:], [R_b[bx]], [XL])
            CP("dve", carry.t[:, c, :], XL.t[:, 512:515], [XL], [carry])
            TS("dve", XC.t[:], XL.t[:, 0:512], cw.t[:, c, 0:1], cb.t[:, c:c + 1], ALU.mult, ALU.add, [XL, cw, cb], [XC])
            for j in range(1, 4):
                STT(XC.t[:], XL.t[:, j:j + 512], cw.t[:, c, j:j + 1], XC.t[:], ALU.mult, ALU.add, [XL, cw, XC], [XC])
            ACT(XCB.t[:], XC.t[:], AF.Copy, [XC], [XCB])
            ACT(GL.t[:], banks[bg][:], AF.Gelu_apprx_tanh, [R_b[bg]], [GL])

        def lru_s2(cs):
            bks = {}
            for c in cs:
                XCB = xcb2[c % 2]
                ba = PN()
                MM(ba, banks[ba][:], bda.t[:, c, :], XCB.t[:], True, True, [bda, XCB])
                bi = PN()
                MM(bi, banks[bi][:], bdx.t[:, c, :], XCB.t[:], True, True, [bdx, XCB])
                bks[c] = (ba, bi)
            for c in cs:
                ba, bi = bks[c]
                ACT(thr2[c % 2].t[:], banks[ba][:], AF.Tanh, [R_b[ba], hba], [thr2[c % 2]], scale=0.5, bias=hba.t[:, c:c + 1])
                ACT(thi2[c % 2].t[:], banks[bi][:], AF.Tanh, [R_b[bi], hbx], [thi2[c % 2]], scale=0.5, bias=hbx.t[:, c:c + 1])
            for c in cs:
                TH, M = thr2[c % 2], m2[c % 2]
                ACT(M.t[:], TH.t[:], AF.Exp, [TH, c8], [M], scale=c8.t[:, c:c + 1], bias=c8.t[:, c:c + 1])
                ACT(TH.t[:], TH.t[:], AF.Exp, [TH, c4], [TH], scale=c4.t[:, c:c + 1], bias=c4.t[:, c:c + 1])
            for c in cs:
                M = m2[c % 2]
                ACT(M.t[:], M.t[:], AF.Sqrt, [M], [M], scale=-1.0, bias=1.0)
            for c in cs:
                XC, GL, TH, TI, M = xc2[c % 2], gl2[c % 2], thr2[c % 2], thi2[c % 2], m2[c % 2]
                STT(TI.t[:], TI.t[:], 1.0, XC.t[:], ALU.add, ALU.mult, [TI, XC], [TI])
                STT(TI.t[:], TI.t[:], 0.5, M.t[:], ALU.mult, ALU.mult, [TI, M], [TI])
                S.op("dve", lambda e, c=c, TH=TH, TI=TI: e.tensor_tensor_scan(out=hl.t[:], data0=TH.t[:], data1=TI.t[:],
                                                                              initial=hstate.t[:, c:c + 1], op0=ALU.mult, op1=ALU.add),
                     [TH, TI, hstate], [hl])
                CP("dve", hstate.t[:, c:c + 1], hl.t[:, 511:512], [hl], [hstate])
                TT("pool", gated.t[:, c, :], GL.t[:], hl.t[:], ALU.mult, [GL, hl], [gated])

        def qk_task(pc, ci):
            def f(w, sl):
                cc = pc * 4 + ci
                b = proj(lambda k: w[:, k, ci * 128:(ci + 1) * 128], sl)
                if cc < 6:
                    ACT(qT.t[:, cc, :], banks[b][:], AF.Copy, [R_b[b]], [qT], scale=0.125)
                else:
                    g, pair = (cc - 6) // 2, (cc - 6) % 2
                    base = (t * T) if g == 2 else par * T
                    ACT(KT[g].t[:, pair, base:base + T], banks[b][:], AF.Copy, [R_b[b]], [KT[g]])
            return f

        def v0_task(j):
            def f(w, sl):
                b = proj(lambda k: hT.t[:, k, j * 128:(j + 1) * 128], hT, rhs=lambda k: w[:, k, 0:256], Rrhs=sl, n=256)
                ACT(V0.t[:, (4 * t + j) % 8, :], banks[b][:, 0:256], AF.Copy, [R_b[b]], [V0])
            return f

        def v1_task(r):
            def f(w, sl):
                b = proj(lambda k: hT.t[:, k, :].rearrange("p (i r) -> p r i", r=4)[:, r, :], hT, rhs=lambda k: w[:, k, 256:512], Rrhs=sl, n=256)
                ACT(V1.t[:, par * 4 + r, :], banks[b][:, 0:256], AF.Copy, [R_b[b]], [V1])
            return f

        def v2_task(j):
            def f(w, sl):
                b = PN()
                for k in range(8):
                    lhs = hT.t[:, k, :].rearrange("p (m f) -> p f m", f=4)[:, j, :]
                    MM(b, banks[b][:, 0:256], lhs, w[:, k, :], k == 0, k == 7, [hT, sl])
                vt = V2tmp[j]
                va = vt.t[:]
                ACT(va, banks[b][:, 0:256], AF.Copy, [R_b[b]], [vt])
                ps_ = va.ap[0][0]
                DMA("act", [(V2.t[p0:p0 + 32, n2 * 16 + 4 * rl + j, :], bass.AP(va.tensor, va.offset + rl * ps_, [[4 * ps_, 32], [1, 256]]))
                           for rl in range(4)], [vt], [V2])
            return f

        fillers = [("QK%d" % pc, qk_task(pc, ci)) for pc in range(3) for ci in range(4)]
        fillers += [("V01", v0_task(j)) for j in range(4)] + [("V01", v1_task(r)) for r in range(4)]
        fillers += [("V2", v2_task(j)) for j in range(4)]
        nfill = [4, 4, 4, 4, 4, 4]
        steps = []
        fi = 0
        for cp in range(3):
            steps.append(("L%d" % cp, lambda w, sl, cp=cp: (lru_s1(cp * 2, 0, w, sl), lru_s1(cp * 2 + 1, 1, w, sl))))
            for ci in range(2):
                c = cp * 2 + ci
                for _ in range(nfill[c]):
                    steps.append(fillers[fi])
                    fi += 1
                if ci == 0:
                    steps.append((None, lambda w, sl, cp=cp: lru_s2((cp * 2, cp * 2 + 1))))
        assert fi == len(fillers)
        S.phase = 'lru'
        fetched = {}
        for (pname, fn) in steps:
            if pname is None:
                fn(None, None)
                continue
            if pname not in fetched:
                fetched[pname] = WGET(pname)
            fn(*fetched[pname])
        if t == 0:
            dump("gated", gated, [128, 6, T], BF16)
        if stop == "gated":
            break
        S.phase = 'mem'
        w, sl = WGET("QM")
        for h in range(4):
            b = proj(lambda k, h=h: w[:, k, h * 128:(h + 1) * 128], sl)
            CP("dve", qmT.t[:, h, :], banks[b][:], [R_b[b]], [qmT])
        if t == 0:
            dump("omT", omT, [128, 4, T], BF16)
        if stop == "omT":
            break

        S.phase = 'dil'
        VV = [V0, V1, V2]

        def q_ap(g, pair, s_, rows):
            q = qT.t[rows, g * 2 + pair, :]
            if g == 0:
                return q[:, s_ * 128:(s_ + 1) * 128]
            return q.rearrange("p (i r) -> p r i", r=DILS[g])[:, s_, :]

        def k_ap(g, pair, part, s_, rows):
            if g == 0:
                col0 = ((4 * t + s_ - (1 - part)) * 128) % 1024
                return kT0.t[rows, pair, col0:col0 + 128]
            if g == 1:
                base = ((t - (1 - part)) % 2) * T
                return kT1.t[rows, pair, base:base + T].rearrange("p (i r) -> p r i", r=4)[:, s_, :]
            base = (n2 - (1 - part)) * 2048
            return kT2.t[rows, pair, base:base + 2048].rearrange("p (i r) -> p r i", r=16)[:, s_, :]

        def v_ap(g, part, s_, h):
            if g == 0:
                blk = (4 * t + s_ - (1 - part)) % 8
            elif g == 1:
                blk = ((t - (1 - part)) % 2) * 4 + s_
            else:
                blk = (n2 - (1 - part)) * 16 + s_
            return VV[g].t[:, blk, h * 64:(h + 1) * 64]

        def has_prev(g, s_):
            if g == 0:
                return 4 * t + s_ > 0
            if g == 1:
                return t > 0
            return n2 > 0

        units = [(pair, g, hh) for pair in range(2) for g in range(3) for hh in range(2)]
        ndb = {}

        def geom(g):
            nb, nq = (4, 128) if g < 2 else (16, 32)
            return nb, nq, (0 if g < 2 else 32 * a2)

        def emit_st(ui):
            pair, g, hh = units[ui]
            nb, nq, qoff = geom(g)
            h = pair * 2 + hh
            rows = slice(hh * 64, hh * 64 + 64)
            for part in range(2):
                sls = [s_ for s_ in range(nb) if part == 1 or has_prev(g, s_)]
                if not sls:
                    continue
                bs = PNS("st")
                for s_ in range(nb):
                    pp = part if s_ in sls else 1
                    MM(bs, banks[bs][:, s_ * nq:(s_ + 1) * nq], k_ap(g, pair, pp, s_, rows), q_ap(g, pair, s_, rows), s_ == 0, False,
                       [KT[g], qT])
                idx = (g * 4 + h) * 2 + part
                bsl = B_all.t[:, idx, qoff:qoff + nq]
                in1 = bass.AP(B_all.t, bsl.offset, [list(bsl.ap[0]), [0, nb], [1, nq]])
                MM(bs, banks[bs][:].rearrange("p (a n) -> p a n", a=nb), ident.t[:], in1, False, True, [ident, B_all])
                ACT(PTs[part][ui % 3].t[:], banks[bs][:], AF.Exp, [R_b[bs]], [PTs[part][ui % 3]])

        def emit_pv(ui):
            pair, g, hh = units[ui]
            nb, nq, qoff = geom(g)
            h = pair * 2 + hh
            orow = slice(hh * 64, hh * 64 + 64)
            if hh == 0:
                ndb[(pair, g)] = (PNS("nd"), PNS("nd"))
            bnum, bden = ndb[(pair, g)]
            for s_ in range(nb):
                parts = [0, 1] if has_prev(g, s_) else [1]
                for ii, part in enumerate(parts):
                    pt = PTs[part][ui % 3]
                    MM(bnum, banks[bnum][orow, s_ * nq:(s_ + 1) * nq], v_ap(g, part, s_, h), pt.t[:, s_ * nq:(s_ + 1) * nq],
                       ii == 0, ii == len(parts) - 1, [VV[g], pt])
            prev_s = [s_ for s_ in range(nb) if has_prev(g, s_)]
            ptc = PTs[1][ui % 3]
            MM(bden, banks[bden][orow, 0:nb * nq], ones.t[:, 0:64], ptc.t[:, 0:nb * nq], True, not prev_s, [ones, ptc])
            if prev_s:
                c0, c1 = prev_s[0] * nq, (prev_s[-1] + 1) * nq
                ptp = PTs[0][ui % 3]
                MM(bden, banks[bden][orow, c0:c1], ones.t[:, 0:64], ptp.t[:, c0:c1], False, True, [ones, ptp])
            if hh == 1:
                if g == 0:
                    ACT(NUM.t[:], banks[bnum][:], AF.Copy, [R_b[bnum]], [NUM])
                    ACT(DEN.t[:], banks[bden][:], AF.Copy, [R_b[bden]], [DEN])
                else:
                    rr = DILS[g]
                    for (dst, bk) in ((NUM, bnum), (DEN, bden)):
                        dv = dst.t[:].rearrange("p (i r) -> p i r", r=rr)
                        TT("dve", dv, dv, banks[bk][:].rearrange("p (r i) -> p i r", r=rr), ALU.add, [dst, R_b[bk]], [dst])
                if g == 2:
                    ACT(rscr.t[:], DEN.t[:], AF.Ln, [DEN], [rscr])
                    ACT(rden.t[:], rscr.t[:], AF.Exp, [rscr], [rden], scale=-1.0)
                    TT("dve", odT.t[:, pair, :], NUM.t[:], rden.t[:], ALU.mult, [NUM, rden], [odT])

        def mem_st(h):
            for kc in range(2):
                bs = PNS("st")
                MM(bs, banks[bs][:], kmT.t[:, h, kc * 128:(kc + 1) * 128], qmT.t[:, h, :], True, True, [kmT, qmT])
                ACT(pTm[kc].t[:], banks[bs][:], AF.Exp, [R_b[bs]], [pTm[kc]], scale=128 ** -0.5)

        def mem_pv(h):
            bo = PNS("st")
            for kc in range(2):
                MM(bo, banks[bo][:], vm.t[:, kc, h * 128:(h + 1) * 128], pTm[kc].t[:], kc == 0, kc == 1, [vm, pTm[kc]])
            bd = PNS("st")
            for kc in range(2):
                MM(bd, banks[bd][:], ones.t[:], pTm[kc].t[:], kc == 0, kc == 1, [ones, pTm[kc]])
            ACT(rscr.t[:], banks[bd][:], AF.Ln, [R_b[bd]], [rscr])
            ACT(rdm.t[:], rscr.t[:], AF.Exp, [rscr], [rdm], scale=-1.0)
            TT("dve", omT.t[:, h, :], banks[bo][:], rdm.t[:], ALU.mult, [R_b[bo], rdm], [omT])

        seq = []
        mh = 0
        for ui in range(len(units)):
            seq.append(("d", ui))
            if ui % 3 == 2:
                seq.append(("m", mh))
                mh += 1

        def do_st(it):
            emit_st(it[1]) if it[0] == "d" else mem_st(it[1])

        def do_pv(it):
            emit_pv(it[1]) if it[0] == "d" else mem_pv(it[1])

        do_st(seq[0])
        do_st(seq[1])
        for si in range(len(seq)):
            if si + 2 < len(seq):
                do_st(seq[si + 2])
            do_pv(seq[si])
        if t == 0:
            dump("odT", odT, [128, 2, T], BF16)
        if stop == "odT":
            break

        S.phase = 'merge'
        opnd = [gated, odT, omT]
        for half in range(2):
            for b3 in range(3):
                wy, sy = WGET("Y%d%d" % (b3, half))
                wg, sgs = WGET("G%d%d" % (b3, half))
                for ci in range(4):
                    c = half * 4 + ci
                    by = proj(lambda k, ci=ci: wy[:, k, ci * 128:(ci + 1) * 128], sy, K=KB[b3],
                              rhs=lambda k, b3=b3: opnd[b3].t[:, k, :], Rrhs=opnd[b3])
                    bgk = proj(lambda k, ci=ci: wg[:, k, ci * 128:(ci + 1) * 128], sgs)
                    sg, tmpm = sgbufs[ci % 2], tmpbufs[ci % 2]
                    accR = R_U[22 + 2 * ci:24 + 2 * ci]
                    mrgR = R_U[30 + c]
                    ACT(sg.t[:], banks[bgk][:], AF.Sigmoid, [R_b[bgk], bgate], [sg], bias=bgate.t[:, b3 * 8 + c:b3 * 8 + c + 1])
                    if b3 == 0:
                        TT("dve", acc.t[:, ci, :], sg.t[:], banks[by][:], ALU.mult, [sg, R_b[by]], [accR])
                    else:
                        TT("dve", tmpm.t[:], sg.t[:], banks[by][:], ALU.mult, [sg, R_b[by]], [tmpm])
                        if b3 == 1:
                            TT("pool", acc.t[:, ci, :], acc.t[:, ci, :], tmpm.t[:], ALU.add, [accR, tmpm], [accR])
                        else:
                            TT("pool", mergedbf.t[:, c, :], acc.t[:, ci, :], tmpm.t[:], ALU.add, [accR, tmpm], [mrgR])
        if t == 0:
            dump("mergedbf", mergedbf, [128, 8, T], BF16)
        if stop == "mergedbf":
            break
        S.phase = 'wout'
        if t + 1 < ntiles:
            load_x(t + 1)
        wo = [WGET("WO0"), WGET("WO1")]
        for s in range(4):
            for half in range(2):
                w, sl = wo[half]
                b = proj(lambda k, s=s: mergedbf.t[:, k, s * 128:(s + 1) * 128], mergedbf, rhs=lambda k: w[:, k, :], Rrhs=sl)
                xs = x_tm.t[:, s, half * 512:(half + 1) * 512]
                TT("dve", xs, xs, banks[b][:], ALU.add, [x_tm.R[s], R_b[b]], [x_tm.R[s]])
        if t == 0:
            dump("x1", x_tm, [128, 4, D])
        if stop == "x1":
            break
        S.phase = 'mlpin'
        norm_T(x_tm, 4, gmlp)
        if t + 1 < ntiles:
            norm_A(x_tms[(t + 1) % 2], 4)
        for pc in range(8):
            w, sl = WGET("MI%d" % pc)
            for ci in range(4):
                hc = pc * 4 + ci
                b = proj(lambda k, ci=ci: w[:, k, ci * 128:(ci + 1) * 128], sl)
                rb = rl[hc % 2]
                ACT(rb.t[:], banks[b][:], AF.Relu, [R_b[b]], [rb])
                TT("pool" if hc % 2 else "dve", uT[hc].t[:], rb.t[:], rb.t[:], ALU.mult, [rb], [uT[hc]])
        if stop == "mlpin":
            break
        if t + 1 < ntiles:
            S.phase = 'norm1'
            norm_B(4, gmix)
        S.phase = 'mlpout'
        for half in range(2):
            accs = [PN() for _ in range(4)]
            for hg in range(4):
                w, sl = WGET("MO%d%d" % (half, hg))
                for hcl in range(8):
                    hc = hg * 8 + hcl
                    for s in range(4):
                        MM(accs[s], banks[accs[s]][:], uT[hc].t[:, s * 128:(s + 1) * 128], w[:, hcl, :], hc == 0, hc == 31, [uT[hc], sl])
            for s in range(4):
                xs = x_tm.t[:, s, half * 512:(half + 1) * 512]
                TT("dve", xs, xs, banks[accs[s]][:], ALU.add, [x_tm, R_b[accs[s]]], [x_tm])
        if stop == "mlpout":
            dump("x2", x_tm, [128, 4, D])
            break
        def final_parts(t=t, x_tm=x_tm):
            def sq(s):
                return lambda: ACT(xn.t[:, s, :], x_tm.t[:, s, :], AF.Square, [x_tm.R[s]], [xn, ss], accum=ss.t[:, s:s + 1])

            def rest():
                TS("dve", rstd.t[:], ss.t[:], 1.0 / D, 1e-6, ALU.mult, ALU.add, [ss], [rstd])
                ACT(rstd.t[:], rstd.t[:], AF.Sqrt, [rstd], [rstd])
                S.op("dve", lambda e: e.reciprocal(out=rstd.t[:], in_=rstd.t[:]), [rstd], [rstd])
                for s in range(4):
                    STT(x_tm.t[:, s, :], x_tm.t[:, s, :], rstd.t[:, s:s + 1], gfin.t[:], ALU.mult, ALU.mult, [x_tm.R[s], rstd, gfin], [x_tm.R[s]])
                    DMA("sp", [(y.ap()[t * T + s * 128:t * T + (s + 1) * 128, :], x_tm.t[:, s, :])], [x_tm.R[s]], [R_ys[s]])
            return [sq(0), sq(1), sq(2), sq(3), rest]

        S.phase = 'final'
        for fn in final_parts():
            fn()

    for fn in pending:
        fn()
    pending.clear()
    S.wait_for("sp", R_ys)
    S.emit()
    global _LAST_SCHED
    _LAST_SCHED = S
    return nc, dbg_out


_LAST_SCHED = None


def kernel(**inputs):
    names = ["g_mix", "w_in", "b_gate", "conv_w", "conv_b", "w_rg_a", "b_rg_a", "w_rg_x", "b_rg_x", "lru_lambda", "w_lru_out",
             "rel_bias", "w_dil_out", "g_mem", "w_mem_kv", "w_mem_out", "w_out", "g_mlp", "w_mlp_in", "w_mlp_out", "g_final"]
    oh, mask = _host_consts()
    shared = {n: np.ascontiguousarray(np.asarray(inputs[n], dtype=np.float32)) for n in names}
    shared["c_oh"] = oh
    shared["c_mask"] = mask
    xs = np.asarray(inputs["x"], dtype=np.float32)
    mems = np.asarray(inputs["mem"], dtype=np.float32)
    nb = xs.shape[0]
    in_maps = []
    for b in range(nb):
        m = dict(shared)
        m["x"] = np.ascontiguousarray(xs[b])
        m["mem"] = np.ascontiguousarray(mems[b])
        in_maps.append(m)
    nc, _ = build_program()
    res = run_bass_kernel_spmd(nc, in_maps, core_ids=list(range(nb)))
    return np.stack([np.asarray(r["y"]) for r in res.results], axis=0).astype(np.float32)
```

```python
import math
import numpy as np
import concourse.bass as bass
import concourse.mybir as mybir
from concourse.bass_utils import run_bass_kernel_spmd

F32 = mybir.dt.float32
BF16 = mybir.dt.bfloat16
AF = mybir.ActivationFunctionType
ALU = mybir.AluOpType

D = 1024
SEQ = 4096
T = 512
NT = SEQ // T
NEG = -1e30
DILS = (1, 4, 16)
SBUF_BYTES = 229376
U_KB = 44
UOFF = SBUF_BYTES - (U_KB + 8) * 1024


class Res:
    __slots__ = ("name", "w", "r", "dsem", "dcnt", "dsem2", "dcnt2")

    def __init__(self, name):
        self.name = name
        self.w = None
        self.r = []
        self.dsem = None
        self.dcnt = 0
        self.dsem2 = None
        self.dcnt2 = 0


class Buf:
    def __init__(self, t, R):
        self.t = t
        self.R = R


def _flat(xs):
    out = []
    for x in xs:
        if isinstance(x, Res):
            out.append(x)
        elif isinstance(x, Buf):
            out.extend(x.R)
        else:
            out.extend(_flat(x))
    return out


class Sched:
    ENG = ("pe", "act", "dve", "pool", "sp")
    SAME_ENG_SYNC = ("act", "dve", "pool")

    def __init__(self, nc):
        self.nc = nc
        self.streams = {e: [] for e in self.ENG}
        self.count = {e: 0 for e in self.ENG}
        self.waited = {e: {} for e in self.ENG}
        self.sem = {}
        for e in self.ENG:
            self.sem["e_" + e] = nc.alloc_semaphore("sem_" + e)
        self.ndsem = 0
        self.phase = 'setup'
        self.tags = {e: [] for e in self.ENG}

    def _collect(self, eng, reads, writes):
        deps = {}

        def need(ev):
            if ev is None:
                return
            s, v = ev
            if deps.get(s, 0) < v:
                deps[s] = v

        for r in reads:
            need(r.w)
        for w in writes:
            need(w.w)
            for ev in w.r:
                need(ev)
        waits = []
        for s, v in deps.items():
            if s == "e_" + eng and eng not in self.SAME_ENG_SYNC:
                continue
            if self.waited[eng].get(s, 0) >= v:
                continue
            self.waited[eng][s] = v
            waits.append((s, v))
        return waits

    def _post(self, ev, reads, writes):
        for r in reads:
            r.r.append(ev)
            if len(r.r) > 12:
                best = {}
                for s, v in r.r:
                    if best.get(s, 0) < v:
                        best[s] = v
                r.r = list(best.items())
        for w in writes:
            w.w = ev
            w.r = []

    def op(self, eng, fn, reads=(), writes=()):
        reads = _flat(reads)
        writes = _flat(writes)
        waits = self._collect(eng, reads, writes)
        self.count[eng] += 1
        ev = ("e_" + eng, self.count[eng])
        self.streams[eng].append((waits, fn, ev[0], 1))
        self.tags[eng].append(self.phase)
        self._post(ev, reads, writes)

    def dma(self, q, fns, reads=(), writes=()):
        reads = _flat(reads)
        writes = _flat(writes)
        dst = writes[0]
        sw = (q == "pool")
        if (dst.dsem2 if sw else dst.dsem) is None:
            nm = "d_%d" % self.ndsem
            self.sem[nm] = self.nc.alloc_semaphore("dsem%d" % self.ndsem)
            self.ndsem += 1
            if sw:
                dst.dsem2 = nm
            else:
                dst.dsem = nm
        waits = self._collect(q, reads, writes)
        if sw:
            dst.dcnt2 += 16 * len(fns)
            ev = (dst.dsem2, dst.dcnt2)
        else:
            dst.dcnt += 16 * len(fns)
            ev = (dst.dsem, dst.dcnt)
        for i, fn in enumerate(fns):
            self.streams[q].append((waits if i == 0 else [], fn, ev[0], 16))
        self._post(ev, reads, writes)

    def wait_for(self, eng, ress):
        waits = self._collect(eng, _flat(ress), ())
        self.streams[eng].append((waits, None, None, 0))

    def emit(self):
        names = {"pe": "tensor", "act": "scalar", "dve": "vector", "pool": "gpsimd", "sp": "sync"}
        with self.nc.Block() as block:
            for eng in self.ENG:
                def f(e, eng=eng):
                    for waits, fn, semname, inc in self.streams[eng]:
                        for (s, v) in waits:
                            e.wait_ge(self.sem[s], v)
                        if fn is not None:
                            fn(e).then_inc(self.sem[semname], inc)
                getattr(block, names[eng])(f)


def _t5_bucket(dist):
    dist = np.asarray(dist, dtype=np.int64)
    df = np.maximum(dist, 1).astype(np.float32)
    val = np.log(df / np.float32(16.0)) / np.float32(math.log(2048 / 16)) * np.float32(16.0)
    large = 16 + val.astype(np.float32).astype(np.int32)
    large = np.minimum(large, 31)
    return np.where(dist < 16, dist, large)


def _host_consts():
    oh = np.zeros((6, 32, 256), np.float32)
    mask = np.zeros((2, 256), np.float32)
    for g, dil in enumerate(DILS):
        for part in range(2):
            for m in range(255):
                d = m - 127
                if part == 0:
                    valid, off = d <= 0, d + 128
                else:
                    valid, off = d >= 0, d
                if valid:
                    oh[g * 2 + part, int(_t5_bucket(off * dil)), m] = 1.0
                else:
                    mask[part, m] = NEG
    return oh, mask


def build_program(ntiles=NT, dbg=(), stop=None):
    nc = bass.Bass("TRN2", target_bir_lowering=False)
    S = Sched(nc)

    def din(name, shape):
        return nc.dram_tensor(name, list(shape), F32, kind="ExternalInput")

    x = din("x", [SEQ, D]); mem = din("mem", [256, D])
    g_mix = din("g_mix", [D]); w_in = din("w_in", [D, 7424]); b_gate = din("b_gate", [3072])
    conv_w = din("conv_w", [4, 768]); conv_b = din("conv_b", [768])
    w_rg_a = din("w_rg_a", [12, 64, 64]); b_rg_a = din("b_rg_a", [768])
    w_rg_x = din("w_rg_x", [12, 64, 64]); b_rg_x = din("b_rg_x", [768])
    lru_lambda = din("lru_lambda", [768]); w_lru_out = din("w_lru_out", [768, D])
    rel_bias = din("rel_bias", [32, 12]); w_dil_out = din("w_dil_out", [256, D])
    g_mem = din("g_mem", [D]); w_mem_kv = din("w_mem_kv", [D, D]); w_mem_out = din("w_mem_out", [512, D])
    w_out = din("w_out", [D, D]); g_mlp = din("g_mlp", [D]); w_mlp_in = din("w_mlp_in", [D, 4096])
    w_mlp_out = din("w_mlp_out", [4096, D]); g_final = din("g_final", [D])
    c_oh = din("c_oh", [6, 32, 256]); c_mask = din("c_mask", [2, 256])
    y = nc.dram_tensor("y", [SEQ, D], F32, kind="ExternalOutput")
    R_y = Res("y")
    R_ys = [Res("y%d" % i) for i in range(4)]
    dbg_out = {}

    def SB(name, shape, dt):
        return Buf(nc.alloc_sbuf_tensor(name, list(shape), dt), [Res(name)])

    x_tms = [SB("x_tm0", [128, 4, D], F32), SB("x_tm1", [128, 4, D], F32)]
    for _xb in x_tms:
        _xb.R = [Res(_xb.R[0].name + "_s%d" % i) for i in range(4)]
    x_tm = x_tms[0]
    xn = SB("xn", [128, 4, D], BF16)
    ss = SB("ss", [128, 4], F32)
    rstd = SB("rstd", [128, 4], F32)
    dmy = SB("dmy", [128, 1], F32)
    hT = SB("hT", [128, 8, T], BF16)
    ident = SB("ident", [128, 128], BF16)
    ones = SB("ones", [128, 128], BF16)
    gmix = SB("gmix", [128, 8], F32); gmem = SB("gmem", [128, 8], F32); gmlp = SB("gmlp", [128, 8], F32)
    bgate = SB("bgate", [128, 24], F32)
    cw = SB("cw", [128, 6, 4], F32); cb = SB("cb", [128, 6], F32)
    ba_c = SB("ba_c", [128, 6], F32); bx_c = SB("bx_c", [128, 6], F32)
    lam = SB("lam", [128, 6], F32); c8 = SB("c8", [128, 6], F32); c4 = SB("c4", [128, 6], F32)
    hba = SB("hba", [128, 6], F32); hbx = SB("hbx", [128, 6], F32)
    gfin = SB("gfin", [128, D], F32)
    bda = SB("bda", [128, 6, 128], BF16); bdx = SB("bdx", [128, 6, 128], BF16)
    carry = SB("carry", [128, 6, 3], F32); hstate = SB("hstate", [128, 6], F32)
    kmT = SB("kmT", [128, 4, 256], BF16); vm = SB("vm", [128, 2, 512], BF16)
    kT0 = SB("kT0", [128, 2, 1024], BF16); kT1 = SB("kT1", [128, 2, 1024], BF16); kT2 = SB("kT2", [128, 2, SEQ], BF16)
    V0 = SB("V0", [128, 8, 256], BF16); V1 = SB("V1", [128, 8, 256], BF16); V2 = SB("V2", [128, 32, 256], BF16)
    B_all = SB("B_all", [128, 24, 128], BF16)
    NSLOT = 5
    wslots = [SB("wslot%d" % i, [128, 4096], BF16) for i in range(NSLOT)]
    assert nc.sbuf_bytes_remaining >= (U_KB + 8) * 1024 + 64, nc.sbuf_bytes_remaining

    R_U = [Res("U%d" % i) for i in range(U_KB)]

    def UB(name, shape, dt, off_kb):
        nbytes = int(np.prod(shape[1:])) * (4 if dt == F32 else 2)
        t = nc.alloc_sbuf_tensor_at(name, list(shape), dt, offset=UOFF + off_kb * 1024)
        return Buf(t, R_U[off_kb: off_kb + (nbytes + 1023) // 1024])

    identf = UB("identf", [128, 128], F32, 0); bdf = UB("bdf", [128, 6, 128], F32, 1)
    ohsb = UB("ohsb", [32, 6, 256], F32, 4); maskb = UB("maskb", [4, 2, 256], F32, 10)
    fsb = UB("fsb", [4, 6, 256], F32, 12); tab = UB("tab", [32, 12], F32, 18)
    gated = UB("gated", [128, 6, T], BF16, 0)
    qT = UB("qT", [128, 6, T], BF16, 6)
    xl2 = [UB("xl0", [128, 515], F32, 12), UB("xl1", [128, 515], F32, 15)]
    xc2 = [UB("xc0", [128, T], F32, 18), UB("xc1", [128, T], F32, 20)]
    xcb2 = [UB("xcb0", [128, T], BF16, 22), UB("xcb1", [128, T], BF16, 23)]
    gl2 = [UB("gl0", [128, T], F32, 24), UB("gl1", [128, T], F32, 26)]
    thr2 = [UB("thr0", [128, T], F32, 28), UB("thr1", [128, T], F32, 30)]
    thi2 = [UB("thi0", [128, T], F32, 32), UB("thi1", [128, T], F32, 34)]
    m2 = [UB("m0", [128, T], F32, 36), UB("m1", [128, T], F32, 38)]
    hl = UB("hl", [128, T], F32, 40)
    V2tmp = [UB("V2tmp%d" % i, [128, 256], BF16, 42) for i in range(1)]
    _v2h = nc.alloc_sbuf_tensor_at("V2tmp_all", [128, 4, 256], BF16, offset=UOFF + 42 * 1024)
    V2tmp = [Buf(_v2h[:, i, :], [R_U[42 + i // 2]]) for i in range(4)]
    qmT = UB("qmT", [128, 4, T], BF16, 12)
    omT = UB("omT", [128, 4, T], BF16, 16)
    odT = UB("odT", [128, 2, T], BF16, 20)
    PTs = [[UB("PT0%d" % i, [128, T], BF16, 22 + i) for i in range(3)],
           [UB("PT1%d" % i, [128, T], BF16, 25 + i) for i in range(3)]]
    NUM = UB("NUM", [128, T], F32, 30); DEN = UB("DEN", [128, T], F32, 32); rden = UB("rden", [128, T], F32, 34)
    pTm = [UB("pTm0", [128, T], BF16, 36), UB("pTm1", [128, T], BF16, 37)]
    rscr = UB("rscr", [128, T], F32, 28)
    rdm = rden
    sgbufs = [UB("sg0", [128, T], F32, 6), UB("sg1", [128, T], F32, 12)]
    tmpbufs = [UB("tmpm0", [128, T], F32, 8), UB("tmpm1", [128, T], F32, 10)]
    acc = UB("acc", [128, 4, T], F32, 22); mergedbf = UB("mergedbf", [128, 8, T], BF16, 30)
    uT = [UB("uT%d" % i, [128, T], BF16, i) for i in range(32)]
    rl = [UB("rl0", [128, T], F32, 32), UB("rl1", [128, T], F32, 34)]

    banks = [nc.alloc_psum_tensor("ps%d" % i, [128, 512], F32) for i in range(8)]
    R_b = [Res("ps%d" % i) for i in range(8)]
    pstate = [0]

    def PN():
        i = pstate[0] % 8
        pstate[0] += 1
        return i

    sub = {"st": [0, (0, 1, 2, 3)], "nd": [0, (4, 5, 6, 7)]}

    def PNS(kind):
        st = sub[kind]
        i = st[1][st[0] % len(st[1])]
        st[0] += 1
        return i

    def ACT(out, in_, func, reads, writes, bias=None, scale=None, accum=None):
        kw = {}
        if bias is not None:
            kw["bias"] = bias
        if scale is not None:
            kw["scale"] = scale
        if accum is not None:
            kw["accum_out"] = accum
        S.op("act", lambda e: e.activation(out=out, in_=in_, func=func, **kw), reads, writes)

    def TS(eng, out, in0, s1, s2, op0, op1, reads, writes):
        S.op(eng, lambda e: e.tensor_scalar(out=out, in0=in0, scalar1=s1, scalar2=s2, op0=op0, op1=op1), reads, writes)

    def TT(eng, out, in0, in1, op, reads, writes):
        S.op(eng, lambda e: e.tensor_tensor(out=out, in0=in0, in1=in1, op=op), reads, writes)

    def STT(out, in0, scalar, in1, op0, op1, reads, writes):
        S.op("dve", lambda e: e.scalar_tensor_tensor(out=out, in0=in0, scalar=scalar, in1=in1, op0=op0, op1=op1), reads, writes)

    def CP(eng, out, in_, reads, writes):
        S.op(eng, lambda e: e.tensor_copy(out, in_), reads, writes)

    def MM(b, out, lhsT, rhs, start, stop, reads):
        S.op("pe", lambda e: e.matmul(out, lhsT=lhsT, rhs=rhs, start=start, stop=stop), reads, [R_b[b]])

    def DMA(q, pairs, reads, writes, slow=False):
        fns = []
        for (o, i) in pairs:
            if slow:
                fns.append(lambda e, o=o, i=i: e.dma_start(out=o, in_=i, allow_slow_non_contiguous=True))
            else:
                fns.append(lambda e, o=o, i=i: e.dma_start(out=o, in_=i))
        S.dma(q, fns, reads, writes)

    S.op("pool", lambda e: e.memset(identf.t[:], 1.0), [], [identf])
    S.op("pool", lambda e: e.affine_select(out=identf.t[:], in_=identf.t[:], pattern=[[-1, 128]], compare_op=ALU.is_equal,
                                           fill=0.0, base=0, channel_multiplier=1), [identf], [identf])
    CP("dve", ident.t[:], identf.t[:], [identf], [ident])
    S.op("dve", lambda e: e.memset(ones.t[:], 1.0), [], [ones])
    for buf in (kT2, V2, carry, hstate, bdf):
        S.op("pool", lambda e, buf=buf: e.memset(buf.t[:], 0.0), [], [buf])

    pieces = {}

    def kp(ap):
        return ap.rearrange("(k p) n -> p k n", p=128)

    def add_piece(name, K, cols, srcs):
        h = nc.dram_tensor("wsc_" + name, [128, K * cols], BF16)
        R = Res("wsc_" + name)
        pieces[name] = (h, K, cols, R, srcs)

    wi = w_in.ap()
    tile_order = []
    add_piece("mkv0", 8, 512, [(0, 512, w_mem_kv.ap()[:, 0:512])])
    add_piece("mkv1", 8, 512, [(0, 512, w_mem_kv.ap()[:, 512:1024])])
    for cp in range(3):
        add_piece("L%d" % cp, 8, 512, [(0, 256, wi[:, cp * 256:(cp + 1) * 256]), (256, 256, wi[:, 768 + cp * 256:768 + (cp + 1) * 256])])
    for pc in range(3):
        add_piece("QK%d" % pc, 8, 512, [(0, 512, wi[:, 1536 + pc * 512:1536 + (pc + 1) * 512])])
    add_piece("V01", 8, 512, [(0, 512, wi[:, 3072:3584])])
    add_piece("V2", 8, 256, [(0, 256, wi[:, 3584:3840])])
    add_piece("QM", 8, 512, [(0, 512, wi[:, 3840:4352])])
    tile_order += ["L0", "QK0", "QK1", "L1", "QK2", "V01", "L2", "V2", "QM"]
    wy_src = [w_lru_out.ap(), w_dil_out.ap(), w_mem_out.ap()]
    KB = [6, 2, 4]
    for half in range(2):
        for b in range(3):
            add_piece("Y%d%d" % (b, half), KB[b], 512, [(0, 512, wy_src[b][:, half * 512:(half + 1) * 512])])
            add_piece("G%d%d" % (b, half), 8, 512, [(0, 512, wi[:, 4352 + b * 1024 + half * 512:4352 + b * 1024 + (half + 1) * 512])])
            tile_order += ["Y%d%d" % (b, half), "G%d%d" % (b, half)]
    for half in range(2):
        add_piece("WO%d" % half, 8, 512, [(0, 512, w_out.ap()[:, half * 512:(half + 1) * 512])]); tile_order.append("WO%d" % half)
    for pc in range(8):
        add_piece("MI%d" % pc, 8, 512, [(0, 512, w_mlp_in.ap()[:, pc * 512:(pc + 1) * 512])]); tile_order.append("MI%d" % pc)
    for half in range(2):
        for hg in range(4):
            add_piece("MO%d%d" % (half, hg), 8, 512, [(0, 512, w_mlp_out.ap()[hg * 1024:(hg + 1) * 1024, half * 512:(half + 1) * 512])])
            tile_order.append("MO%d%d" % (half, hg))
    gseq = ["mkv0", "mkv1"] + tile_order * ntiles
    wst = {"issued": 0, "next": 0}

    seen = set()

    def WGET(expect):
        idx = wst["next"]
        assert gseq[idx] == expect, (gseq[idx], expect)
        wst["next"] += 1
        while wst["issued"] < len(gseq) and wst["issued"] <= idx + NSLOT - 2:
            j = wst["issued"]
            name = gseq[j]
            h, K, cols, R, srcs = pieces[name]
            slot = wslots[j % NSLOT]
            if name not in seen:
                seen.add(name)
                sv = slot.t[:, 0:K * cols].rearrange("p (k n) -> p k n", k=K)
                DMA("pool", [(sv[:, :, o:o + n], kp(src)) for (o, n, src) in srcs], [], [slot])
                if ntiles > 1 and not name.startswith("mkv"):
                    DMA("sp", [(h.ap(), slot.t[:, 0:K * cols])], [slot], [R])
            else:
                DMA("sp", [(slot.t[:, 0:K * cols], h.ap())], [R], [slot])
            wst["issued"] += 1
        h, K, cols, R, srcs = pieces[expect]
        slot = wslots[idx % NSLOT]
        return slot.t[:, 0:K * cols].rearrange("p (k n) -> p k n", k=K), slot

    def colvec(dst, src, nchunk):
        DMA("sp", [(dst.t[:, 0:nchunk], src.ap().rearrange("(c p) -> p c", p=128))], [], [dst], slow=True)

    colvec(gmem, g_mem, 8)
    xm = x_tms[1]
    DMA("sp", [(xm.t[:, 0:2, :], mem.ap().rearrange("(s p) d -> p s d", p=128))], [], [xm])
    colvec(gmix, g_mix, 8)
    if ntiles > 0:
        DMA("sp", [(x_tms[0].t[:], x.ap()[0:T, :].rearrange("(s p) d -> p s d", p=128))], [], [x_tms[0]])
    colvec(gmlp, g_mlp, 8); colvec(bgate, b_gate, 24)
    colvec(cb, conv_b, 6); colvec(ba_c, b_rg_a, 6); colvec(bx_c, b_rg_x, 6); colvec(lam, lru_lambda, 6)
    DMA("sp", [(cw.t[:, :, j], conv_w.ap()[j, :].rearrange("(c p) -> p c", p=128)) for j in range(4)], [], [cw], slow=True)
    DMA("sp", [(gfin.t[:], bass.AP(g_final, 0, [[0, 128], [1, D]]))], [], [gfin])
    ACT(c8.t[:], lam.t[:], AF.Exp, [lam], [c8], scale=-1.0)
    ACT(c8.t[:], c8.t[:], AF.Ln, [c8], [c8], bias=1.0)
    TS("dve", c4.t[:], c8.t[:], -4.0, None, ALU.mult, ALU.bypass, [c8], [c4])
    TS("dve", c8.t[:], c8.t[:], -8.0, None, ALU.mult, ALU.bypass, [c8], [c8])
    TS("dve", hba.t[:], ba_c.t[:], 0.5, None, ALU.mult, ALU.bypass, [ba_c], [hba])
    TS("dve", hbx.t[:], bx_c.t[:], 0.5, None, ALU.mult, ALU.bypass, [bx_c], [hbx])
    for (wsrc, dstb) in ((w_rg_a, bda), (w_rg_x, bdx)):
        v = wsrc.ap().rearrange("(c two) i j -> two i c j", two=2)
        DMA("sp", [(bdf.t[0:64, :, 0:64], v[0]), (bdf.t[64:128, :, 64:128], v[1])], [], [bdf])
        CP("dve", dstb.t[:], bdf.t[:], [bdf], [dstb])

    def norm_A(src, nsub):
        for s in range(nsub):
            ACT(xn.t[:, s, :], src.t[:, s, :], AF.Square, [src.R[s] if len(src.R) == 4 else src], [xn, ss], accum=ss.t[:, s:s + 1])
        TS("dve", rstd.t[:, 0:nsub], ss.t[:, 0:nsub], 1.0 / D, 1e-6, ALU.mult, ALU.add, [ss], [rstd])
        ACT(rstd.t[:, 0:nsub], rstd.t[:, 0:nsub], AF.Sqrt, [rstd], [rstd])
        S.op("dve", lambda e: e.reciprocal(out=rstd.t[:, 0:nsub], in_=rstd.t[:, 0:nsub]), [rstd], [rstd])
        for s in range(nsub):
            if s % 2 == 0:
                ACT(xn.t[:, s, :], src.t[:, s, :], AF.Copy, [src, rstd], [xn], scale=rstd.t[:, s:s + 1])
            else:
                TS("dve", xn.t[:, s, :], src.t[:, s, :], rstd.t[:, s:s + 1], None, ALU.mult, ALU.bypass, [src, rstd], [xn])

    def norm_B(nsub, gcol):
        for c in range(8):
            b = PN()
            pst = banks[b][:].bitcast(BF16)
            for s in range(nsub):
                S.op("pe", lambda e, s=s, c=c, pst=pst: e.transpose(out=pst[:, s * 128:(s + 1) * 128], in_=xn.t[:, s, c * 128:(c + 1) * 128],
                                                                    identity=ident.t[:]), [xn, ident], [R_b[b]])
            n = nsub * 128
            if c % 2 == 0:
                ACT(hT.t[:, c, 0:n], pst[:, 0:n], AF.Copy, [R_b[b], gcol], [hT], scale=gcol.t[:, c:c + 1])
            else:
                TS("dve", hT.t[:, c, 0:n], pst[:, 0:n], gcol.t[:, c:c + 1], None, ALU.mult, ALU.bypass, [R_b[b], gcol], [hT])

    def norm_T(src, nsub, gcol):
        norm_A(src, nsub)
        norm_B(nsub, gcol)

    def proj(lw, Rw, K=8, rhs=None, Rrhs=None, n=T):
        if rhs is None:
            rhs = lambda k: hT.t[:, k, :]
            Rrhs = hT
        b = PN()
        for k in range(K):
            MM(b, banks[b][:, 0:n], lw(k), rhs(k), k == 0, k == K - 1, [Rw, Rrhs])
        return b

    def load_x(tt):
        DMA("sp", [(x_tms[tt % 2].t[:], x.ap()[tt * T:(tt + 1) * T, :].rearrange("(s p) d -> p s d", p=128))], [], [x_tms[tt % 2]])

    DMA("sp", [(tab.t[:], rel_bias.ap())], [], [tab])
    DMA("sp", [(ohsb.t[:], c_oh.ap().rearrange("g b m -> b g m"))], [], [ohsb])
    DMA("sp", [(maskb.t[:], bass.AP(c_mask, 0, [[0, 4], [256, 2], [1, 256]]))], [], [maskb])

    frow = nc.dram_tensor("frow", [6, 4, 256], F32)
    R_frow = Res("frow")
    for gp in range(6):
        g, part = gp // 2, gp % 2
        b = PN()
        MM(b, banks[b][0:4, 0:256], tab.t[:, g * 4:(g + 1) * 4], ohsb.t[:, gp, :], True, True, [tab, ohsb])
        TT("dve", fsb.t[:, gp, :], banks[b][0:4, 0:256], maskb.t[:, part, :], ALU.add, [R_b[b], maskb], [fsb])
    norm_T(xm, 2, gmem)
    w0, s0 = WGET("mkv0")
    for h in range(4):
        b = proj(lambda k, h=h: w0[:, k, h * 128:(h + 1) * 128], s0, rhs=lambda k: hT.t[:, k, 0:256], Rrhs=hT, n=256)
        CP("dve", kmT.t[:, h, :], banks[b][:, 0:256], [R_b[b]], [kmT])
    w1, s1 = WGET("mkv1")
    for kc in range(2):
        b = proj(lambda k, kc=kc: hT.t[:, k, kc * 128:(kc + 1) * 128], hT, rhs=lambda k: w1[:, k, :], Rrhs=s1)
        CP("dve", vm.t[:, kc, :], banks[b][:], [R_b[b]], [vm])

    DMA("sp", [(frow.ap().rearrange("g h m -> h g m"), fsb.t[:])], [fsb], [R_frow])
    frep = nc.dram_tensor("frep", [24, 128, 256], BF16)
    R_frep = Res("frep")
    DMA("pool", [(frep.ap(), bass.AP(frow, 0, [[256, 24], [0, 128], [1, 256]]))], [R_frow], [R_frep])
    pairs = []
    for g in range(3):
        for h in range(4):
            for part in range(2):
                gp = g * 2 + part
                off = ((gp * 4 + h) * 128) * 256 + 127
                pairs.append((B_all.t[:, (g * 4 + h) * 2 + part, :], bass.AP(frep, off, [[255, 128], [1, 128]])))
    DMA("sp", pairs, [R_frep], [B_all])

    def dump(name, buf, shape, dt=F32):
        if name in dbg and name not in dbg_out:
            h = nc.dram_tensor("dbg_" + name, list(shape), dt, kind="ExternalOutput")
            dbg_out[name] = h
            DMA("sp", [(h.ap(), buf.t[:])], [buf], [Res("dbg_" + name)])
            S.wait_for("sp", [Res("dbg_" + name)])

    dump("B_all", B_all, [128, 24, 128])
    dump("kmT", kmT, [128, 4, 256], BF16)
    dump("vm", vm, [128, 2, 512], BF16)

    if ntiles > 0:
        S.phase = 'norm1'
        norm_T(x_tms[0], 4, gmix)
    pending = []
    for t in range(ntiles):
        par = t % 2
        n2, a2 = t // 4, t % 4
        x_tm = x_tms[t % 2]
        if t == 0:
            dump("hT", hT, [128, 8, T], BF16)
        if stop == "hT":
            break

        KT = [kT0, kT1, kT2]
        p0 = 32 * a2

        def lru_s1(c, ci, w, sl):
            XL, XC, XCB, GL = xl2[c % 2], xc2[c % 2], xcb2[c % 2], gl2[c % 2]
            bx = proj(lambda k: w[:, k, ci * 128:(ci + 1) * 128], sl)
            bg = proj(lambda k: w[:, k, (2 + ci) * 128:(3 + ci) * 128], sl)
            CP("dve", XL.t[:, 0:3], carry.t[:, c, :], [carry], [XL])
            CP("dve", XL.t[:, 3:515], banks[bx][:], [R_b[bx]], [XL])
            CP("dve", carry.t[:, c, :], XL.t[:, 512:515], [XL], [carry])
            TS("dve", XC.t[:], XL.t[:, 0:512], cw.t[:, c, 0:1], cb.t[:, c:c + 1], ALU.mult, ALU.add, [XL, cw, cb], [XC])
            for j in range(1, 4):
                STT(XC.t[:], XL.t[:, j:j + 512], cw.t[:, c, j:j + 1], XC.t[:], ALU.mult, ALU.add, [XL, cw, XC], [XC])
            ACT(XCB.t[:], XC.t[:], AF.Copy, [XC], [XCB])
            ACT(GL.t[:], banks[bg][:], AF.Gelu_apprx_tanh, [R_b[bg]], [GL])

        def lru_s2(cs):
            bks = {}
            for c in cs:
                XCB = xcb2[c % 2]
                ba = PN()
                MM(ba, banks[ba][:], bda.t[:, c, :], XCB.t[:], True, True, [bda, XCB])
                bi = PN()
                MM(bi, banks[bi][:], bdx.t[:, c, :], XCB.t[:], True, True, [bdx, XCB])
                bks[c] = (ba, bi)
            for c in cs:
                ba, bi = bks[c]
                ACT(thr2[c % 2].t[:], banks[ba][:], AF.Tanh, [R_b[ba], hba], [thr2[c % 2]], scale=0.5, bias=hba.t[:, c:c + 1])
                ACT(thi2[c % 2].t[:], banks[bi][:], AF.Tanh, [R_b[bi], hbx], [thi2[c % 2]], scale=0.5, bias=hbx.t[:, c:c + 1])
            for c in cs:
                TH, M = thr2[c % 2], m2[c % 2]
                ACT(M.t[:], TH.t[:], AF.Exp, [TH, c8], [M], scale=c8.t[:, c:c + 1], bias=c8.t[:, c:c + 1])
                ACT(TH.t[:], TH.t[:], AF.Exp, [TH, c4], [TH], scale=c4.t[:, c:c + 1], bias=c4.t[:, c:c + 1])
            for c in cs:
                M = m2[c % 2]
                ACT(M.t[:], M.t[:], AF.Sqrt, [M], [M], scale=-1.0, bias=1.0)
            for c in cs:
                XC, GL, TH, TI, M = xc2[c % 2], gl2[c % 2], thr2[c % 2], thi2[c % 2], m2[c % 2]
                STT(TI.t[:], TI.t[:], 1.0, XC.t[:], ALU.add, ALU.mult, [TI, XC], [TI])
                STT(TI.t[:], TI.t[:], 0.5, M.t[:], ALU.mult, ALU.mult, [TI, M], [TI])
                S.op("dve", lambda e, c=c, TH=TH, TI=TI: e.tensor_tensor_scan(out=hl.t[:], data0=TH.t[:], data1=TI.t[:],
                                                                              initial=hstate.t[:, c:c + 1], op0=ALU.mult, op1=ALU.add),
                     [TH, TI, hstate], [hl])
                CP("dve", hstate.t[:, c:c + 1], hl.t[:, 511:512], [hl], [hstate])
                TT("pool", gated.t[:, c, :], GL.t[:], hl.t[:], ALU.mult, [GL, hl], [gated])

        def qk_task(pc, ci):
            def f(w, sl):
                cc = pc * 4 + ci
                b = proj(lambda k: w[:, k, ci * 128:(ci + 1) * 128], sl)
                if cc < 6:
                    ACT(qT.t[:, cc, :], banks[b][:], AF.Copy, [R_b[b]], [qT], scale=0.125)
                else:
                    g, pair = (cc - 6) // 2, (cc - 6) % 2
                    base = (t * T) if g == 2 else par * T
                    ACT(KT[g].t[:, pair, base:base + T], banks[b][:], AF.Copy, [R_b[b]], [KT[g]])
            return f

        def v0_task(j):
            def f(w, sl):
                b = proj(lambda k: hT.t[:, k, j * 128:(j + 1) * 128], hT, rhs=lambda k: w[:, k, 0:256], Rrhs=sl, n=256)
                ACT(V0.t[:, (4 * t + j) % 8, :], banks[b][:, 0:256], AF.Copy, [R_b[b]], [V0])
            return f

        def v1_task(r):
            def f(w, sl):
                b = proj(lambda k: hT.t[:, k, :].rearrange("p (i r) -> p r i", r=4)[:, r, :], hT, rhs=lambda k: w[:, k, 256:512], Rrhs=sl, n=256)
                ACT(V1.t[:, par * 4 + r, :], banks[b][:, 0:256], AF.Copy, [R_b[b]], [V1])
            return f

        def v2_task(j):
            def f(w, sl):
                b = PN()
                for k in range(8):
                    lhs = hT.t[:, k, :].rearrange("p (m f) -> p f m", f=4)[:, j, :]
                    MM(b, banks[b][:, 0:256], lhs, w[:, k, :], k == 0, k == 7, [hT, sl])
                vt = V2tmp[j]
                va = vt.t[:]
                ACT(va, banks[b][:, 0:256], AF.Copy, [R_b[b]], [vt])
                ps_ = va.ap[0][0]
                DMA("act", [(V2.t[p0:p0 + 32, n2 * 16 + 4 * rl + j, :], bass.AP(va.tensor, va.offset + rl * ps_, [[4 * ps_, 32], [1, 256]]))
                           for rl in range(4)], [vt], [V2])
            return f

        fillers = [("QK%d" % pc, qk_task(pc, ci)) for pc in range(3) for ci in range(4)]
        fillers += [("V01", v0_task(j)) for j in range(4)] + [("V01", v1_task(r)) for r in range(4)]
        fillers += [("V2", v2_task(j)) for j in range(4)]
        nfill = [4, 4, 4, 4, 4, 4]
        steps = []
        fi = 0
        for cp in range(3):
            steps.append(("L%d" % cp, lambda w, sl, cp=cp: (lru_s1(cp * 2, 0, w, sl), lru_s1(cp * 2 + 1, 1, w, sl))))
            for ci in range(2):
                c = cp * 2 + ci
                for _ in range(nfill[c]):
                    steps.append(fillers[fi])
                    fi += 1
                if ci == 0:
                    steps.append((None, lambda w, sl, cp=cp: lru_s2((cp * 2, cp * 2 + 1))))
        assert fi == len(fillers)
        S.phase = 'lru'
        fetched = {}
        for (pname, fn) in steps:
            if pname is None:
                fn(None, None)
                continue
            if pname not in fetched:
                fetched[pname] = WGET(pname)
            fn(*fetched[pname])
        if t == 0:
            dump("gated", gated, [128, 6, T], BF16)
        if stop == "gated":
            break
        S.phase = 'mem'
        w, sl = WGET("QM")
        for h in range(4):
            b = proj(lambda k, h=h: w[:, k, h * 128:(h + 1) * 128], sl)
            CP("dve", qmT.t[:, h, :], banks[b][:], [R_b[b]], [qmT])
        if t == 0:
            dump("omT", omT, [128, 4, T], BF16)
        if stop == "omT":
            break

        S.phase = 'dil'
        VV = [V0, V1, V2]

        def q_ap(g, pair, s_, rows):
            q = qT.t[rows, g * 2 + pair, :]
            if g == 0:
                return q[:, s_ * 128:(s_ + 1) * 128]
            return q.rearrange("p (i r) -> p r i", r=DILS[g])[:, s_, :]

        def k_ap(g, pair, part, s_, rows):
            if g == 0:
                col0 = ((4 * t + s_ - (1 - part)) * 128) % 1024
                return kT0.t[rows, pair, col0:col0 + 128]
            if g == 1:
                base = ((t - (1 - part)) % 2) * T
                return kT1.t[rows, pair, base:base + T].rearrange("p (i r) -> p r i", r=4)[:, s_, :]
            base = (n2 - (1 - part)) * 2048
            return kT2.t[rows, pair, base:base + 2048].rearrange("p (i r) -> p r i", r=16)[:, s_, :]

        def v_ap(g, part, s_, h):
            if g == 0:
                blk = (4 * t + s_ - (1 - part)) % 8
            elif g == 1:
                blk = ((t - (1 - part)) % 2) * 4 + s_
            else:
                blk = (n2 - (1 - part)) * 16 + s_
            return VV[g].t[:, blk, h * 64:(h + 1) * 64]

        def has_prev(g, s_):
            if g == 0:
                return 4 * t + s_ > 0
            if g == 1:
                return t > 0
            return n2 > 0

        units = [(pair, g, hh) for pair in range(2) for g in range(3) for hh in range(2)]
        ndb = {}

        def geom(g):
            nb, nq = (4, 128) if g < 2 else (16, 32)
            return nb, nq, (0 if g < 2 else 32 * a2)

        def emit_st(ui):
            pair, g, hh = units[ui]
            nb, nq, qoff = geom(g)
            h = pair * 2 + hh
            rows = slice(hh * 64, hh * 64 + 64)
            for part in range(2):
                sls = [s_ for s_ in range(nb) if part == 1 or has_prev(g, s_)]
                if not sls:
                    continue
                bs = PNS("st")
                for s_ in range(nb):
                    pp = part if s_ in sls else 1
                    MM(bs, banks[bs][:, s_ * nq:(s_ + 1) * nq], k_ap(g, pair, pp, s_, rows), q_ap(g, pair, s_, rows), s_ == 0, False,
                       [KT[g], qT])
                idx = (g * 4 + h) * 2 + part
                bsl = B_all.t[:, idx, qoff:qoff + nq]
                in1 = bass.AP(B_all.t, bsl.offset, [list(bsl.ap[0]), [0, nb], [1, nq]])
                MM(bs, banks[bs][:].rearrange("p (a n) -> p a n", a=nb), ident.t[:], in1, False, True, [ident, B_all])
                ACT(PTs[part][ui % 3].t[:], banks[bs][:], AF.Exp, [R_b[bs]], [PTs[part][ui % 3]])

        def emit_pv(ui):
            pair, g, hh = units[ui]
            nb, nq, qoff = geom(g)
            h = pair * 2 + hh
            orow = slice(hh * 64, hh * 64 + 64)
            if hh == 0:
                ndb[(pair, g)] = (PNS("nd"), PNS("nd"))
            bnum, bden = ndb[(pair, g)]
            for s_ in range(nb):
                parts = [0, 1] if has_prev(g, s_) else [1]
                for ii, part in enumerate(parts):
                    pt = PTs[part][ui % 3]
                    MM(bnum, banks[bnum][orow, s_ * nq:(s_ + 1) * nq], v_ap(g, part, s_, h), pt.t[:, s_ * nq:(s_ + 1) * nq],
                       ii == 0, ii == len(parts) - 1, [VV[g], pt])
            prev_s = [s_ for s_ in range(nb) if has_prev(g, s_)]
            ptc = PTs[1][ui % 3]
            MM(bden, banks[bden][orow, 0:nb * nq], ones.t[:, 0:64], ptc.t[:, 0:nb * nq], True, not prev_s, [ones, ptc])
            if prev_s:
                c0, c1 = prev_s[0] * nq, (prev_s[-1] + 1) * nq
                ptp = PTs[0][ui % 3]
                MM(bden, banks[bden][orow, c0:c1], ones.t[:, 0:64], ptp.t[:, c0:c1], False, True, [ones, ptp])
            if hh == 1:
                if g == 0:
                    ACT(NUM.t[:], banks[bnum][:], AF.Copy, [R_b[bnum]], [NUM])
                    ACT(DEN.t[:], banks[bden][:], AF.Copy, [R_b[bden]], [DEN])
                else:
                    rr = DILS[g]
                    for (dst, bk) in ((NUM, bnum), (DEN, bden)):
                        dv = dst.t[:].rearrange("p (i r) -> p i r", r=rr)
                        TT("dve", dv, dv, banks[bk][:].rearrange("p (r i) -> p i r", r=rr), ALU.add, [dst, R_b[bk]], [dst])
                if g == 2:
                    ACT(rscr.t[:], DEN.t[:], AF.Ln, [DEN], [rscr])
                    ACT(rden.t[:], rscr.t[:], AF.Exp, [rscr], [rden], scale=-1.0)
                    TT("dve", odT.t[:, pair, :], NUM.t[:], rden.t[:], ALU.mult, [NUM, rden], [odT])

        def mem_st(h):
            for kc in range(2):
                bs = PNS("st")
                MM(bs, banks[bs][:], kmT.t[:, h, kc * 128:(kc + 1) * 128], qmT.t[:, h, :], True, True, [kmT, qmT])
                ACT(pTm[kc].t[:], banks[bs][:], AF.Exp, [R_b[bs]], [pTm[kc]], scale=128 ** -0.5)

        def mem_pv(h):
            bo = PNS("st")
            for kc in range(2):
                MM(bo, banks[bo][:], vm.t[:, kc, h * 128:(h + 1) * 128], pTm[kc].t[:], kc == 0, kc == 1, [vm, pTm[kc]])
            bd = PNS("st")
            for kc in range(2):
                MM(bd, banks[bd][:], ones.t[:], pTm[kc].t[:], kc == 0, kc == 1, [ones, pTm[kc]])
            ACT(rscr.t[:], banks[bd][:], AF.Ln, [R_b[bd]], [rscr])
            ACT(rdm.t[:], rscr.t[:], AF.Exp, [rscr], [rdm], scale=-1.0)
            TT("dve", omT.t[:, h, :], banks[bo][:], rdm.t[:], ALU.mult, [R_b[bo], rdm], [omT])

        seq = []
        mh = 0
        for ui in range(len(units)):
            seq.append(("d", ui))
            if ui % 3 == 2:
                seq.append(("m", mh))
                mh += 1

        def do_st(it):
            emit_st(it[1]) if it[0] == "d" else mem_st(it[1])

        def do_pv(it):
            emit_pv(it[1]) if it[0] == "d" else mem_pv(it[1])

        do_st(seq[0])
        do_st(seq[1])
        for si in range(len(seq)):
            if si + 2 < len(seq):
                do_st(seq[si + 2])
            do_pv(seq[si])
        if t == 0:
            dump("odT", odT, [128, 2, T], BF16)
        if stop == "odT":
            break

        S.phase = 'merge'
        opnd = [gated, odT, omT]
        for half in range(2):
            for b3 in range(3):
                wy, sy = WGET("Y%d%d" % (b3, half))
                wg, sgs = WGET("G%d%d" % (b3, half))
                for ci in range(4):
                    c = half * 4 + ci
                    by = proj(lambda k, ci=ci: wy[:, k, ci * 128:(ci + 1) * 128], sy, K=KB[b3],
                              rhs=lambda k, b3=b3: opnd[b3].t[:, k, :], Rrhs=opnd[b3])
                    bgk = proj(lambda k, ci=ci: wg[:, k, ci * 128:(ci + 1) * 128], sgs)
                    sg, tmpm = sgbufs[ci % 2], tmpbufs[ci % 2]
                    accR = R_U[22 + 2 * ci:24 + 2 * ci]
                    mrgR = R_U[30 + c]
                    ACT(sg.t[:], banks[bgk][:], AF.Sigmoid, [R_b[bgk], bgate], [sg], bias=bgate.t[:, b3 * 8 + c:b3 * 8 + c + 1])
                    if b3 == 0:
                        TT("dve", acc.t[:, ci, :], sg.t[:], banks[by][:], ALU.mult, [sg, R_b[by]], [accR])
                    else:
                        TT("dve", tmpm.t[:], sg.t[:], banks[by][:], ALU.mult, [sg, R_b[by]], [tmpm])
                        if b3 == 1:
                            TT("pool", acc.t[:, ci, :], acc.t[:, ci, :], tmpm.t[:], ALU.add, [accR, tmpm], [accR])
                        else:
                            TT("pool", mergedbf.t[:, c, :], acc.t[:, ci, :], tmpm.t[:], ALU.add, [accR, tmpm], [mrgR])
        if t == 0:
            dump("mergedbf", mergedbf, [128, 8, T], BF16)
        if stop == "mergedbf":
            break
        ACT(dmy.t[:], ss.t[:, 0:1], AF.Sqrt, [ss], [dmy])
        S.phase = 'wout'
        if t + 1 < ntiles:
            load_x(t + 1)
        wo = [WGET("WO0"), WGET("WO1")]
        for s in range(4):
            for half in range(2):
                w, sl = wo[half]
                b = proj(lambda k, s=s: mergedbf.t[:, k, s * 128:(s + 1) * 128], mergedbf, rhs=lambda k: w[:, k, :], Rrhs=sl)
                xs = x_tm.t[:, s, half * 512:(half + 1) * 512]
                TT("dve", xs, xs, banks[b][:], ALU.add, [x_tm.R[s], R_b[b]], [x_tm.R[s]])
        if t == 0:
            dump("x1", x_tm, [128, 4, D])
        if stop == "x1":
            break
        S.phase = 'mlpin'
        norm_T(x_tm, 4, gmlp)
        if t + 1 < ntiles:
            norm_A(x_tms[(t + 1) % 2], 4)
        for pc in range(8):
            w, sl = WGET("MI%d" % pc)
            for ci in range(4):
                hc = pc * 4 + ci
                b = proj(lambda k, ci=ci: w[:, k, ci * 128:(ci + 1) * 128], sl)
                rb = rl[hc % 2]
                ACT(rb.t[:], banks[b][:], AF.Relu, [R_b[b]], [rb])
                TT("pool" if hc % 2 else "dve", uT[hc].t[:], rb.t[:], rb.t[:], ALU.mult, [rb], [uT[hc]])
        if stop == "mlpin":
            break
        if t + 1 < ntiles:
            S.phase = 'norm1'
            norm_B(4, gmix)
        S.phase = 'mlpout'
        for half in range(2):
            accs = [PN() for _ in range(4)]
            for hg in range(4):
                w, sl = WGET("MO%d%d" % (half, hg))
                for hcl in range(8):
                    hc = hg * 8 + hcl
                    for s in range(4):
                        MM(accs[s], banks[accs[s]][:], uT[hc].t[:, s * 128:(s + 1) * 128], w[:, hcl, :], hc == 0, hc == 31, [uT[hc], sl])
            for s in range(4):
                xs = x_tm.t[:, s, half * 512:(half + 1) * 512]
                TT("dve", xs, xs, banks[accs[s]][:], ALU.add, [x_tm, R_b[accs[s]]], [x_tm])
        if stop == "mlpout":
            dump("x2", x_tm, [128, 4, D])
            break
        def final_parts(t=t, x_tm=x_tm):
            def sq(s):
                return lambda: ACT(xn.t[:, s, :], x_tm.t[:, s, :], AF.Square, [x_tm.R[s]], [xn, ss], accum=ss.t[:, s:s + 1])

            def rest():
                TS("dve", rstd.t[:], ss.t[:], 1.0 / D, 1e-6, ALU.mult, ALU.add, [ss], [rstd])
                ACT(rstd.t[:], rstd.t[:], AF.Sqrt, [rstd], [rstd])
                S.op("dve", lambda e: e.reciprocal(out=rstd.t[:], in_=rstd.t[:]), [rstd], [rstd])
                for s in range(4):
                    STT(x_tm.t[:, s, :], x_tm.t[:, s, :], rstd.t[:, s:s + 1], gfin.t[:], ALU.mult, ALU.mult, [x_tm.R[s], rstd, gfin], [x_tm.R[s]])
                    DMA("sp", [(y.ap()[t * T + s * 128:t * T + (s + 1) * 128, :], x_tm.t[:, s, :])], [x_tm.R[s]], [R_ys[s]])
            return [sq(0), sq(1), sq(2), sq(3), rest]

        S.phase = 'final'
        for fn in final_parts():
            fn()

    for fn in pending:
        fn()
    pending.clear()
    S.wait_for("sp", R_ys)
    S.emit()
    global _LAST_SCHED
    _LAST_SCHED = S
    return nc, dbg_out


_LAST_SCHED = None


def kernel(**inputs):
    names = ["g_mix", "w_in", "b_gate", "conv_w", "conv_b", "w_rg_a", "b_rg_a", "w_rg_x", "b_rg_x", "lru_lambda", "w_lru_out",
             "rel_bias", "w_dil_out", "g_mem", "w_mem_kv", "w_mem_out", "w_out", "g_mlp", "w_mlp_in", "w_mlp_out", "g_final"]
    oh, mask = _host_consts()
    shared = {n: np.ascontiguousarray(np.asarray(inputs[n], dtype=np.float32)) for n in names}
    shared["c_oh"] = oh
    shared["c_mask"] = mask
    xs = np.asarray(inputs["x"], dtype=np.float32)
    mems = np.asarray(inputs["mem"], dtype=np.float32)
    nb = xs.shape[0]
    in_maps = []
    for b in range(nb):
        m = dict(shared)
        m["x"] = np.ascontiguousarray(xs[b])
        m["mem"] = np.ascontiguousarray(mems[b])
        in_maps.append(m)
    nc, _ = build_program()
    res = run_bass_kernel_spmd(nc, in_maps, core_ids=list(range(nb)))
    return np.stack([np.asarray(r["y"]) for r in res.results], axis=0).astype(np.float32)
```
